# Optimizing a Trainium2 kernel written in Bass

```python
import math
import jax, jax.numpy as jnp
from jax import lax
import numpy as np

D_MODEL = 2048
BATCH = 16
SEQ = 256
DEPTH = 4
DEC_BATCH = 2
DEC_SEQ = 1024
PAST_LEN = 256

GRID_W = 64
N_EVEN = (DEPTH + 1) // 2
N_ODD = DEPTH // 2
EPS = 1e-6
H_A = 8
DK_A = 128
DV_A = 128
CONV_K = 5
DELTA_CHUNK = 64
H_B = 8
DQK_B = 64
DV_B = 2 * DQK_B
Q_BLOCK = 128
ROPE_BASE = 10000.0
H_C = 4
DK_C = D_MODEL // 2 // H_C
DV_C = D_MODEL // H_C
GATE_RANK = 16
GLA_NORMALIZER = 16.0
GLA_CHUNK = 16
WIDTHS_EVEN = (H_A * DK_A, H_A * DK_A, H_A * DV_A, H_A * DV_A, 2 * H_A, 2 * H_A,
               H_B * 2 * DQK_B, H_B * 2 * DQK_B, H_B * DV_B, H_B * DV_B)
D_IN_EVEN = sum(WIDTHS_EVEN)
MIX_EVEN = H_A * DV_A + H_B * DV_B
WIDTHS_ODD = (H_C * DK_C, H_C * DK_C, H_C * DV_C, H_C * DV_C, 2 * GATE_RANK)
D_IN_ODD = sum(WIDTHS_ODD)
MIX_ODD = H_C * DV_C

kernel_name = 'hybrid_dit_prefix_step'


def _split(x, widths):
    out, start = [], 0
    for w in widths:
        out.append(x[..., start:start + w])
        start += w
    return out


def rms_norm(x, g):
    xf = x.astype(jnp.float32)
    y = xf * lax.rsqrt(jnp.mean(xf * xf, axis=-1, keepdims=True) + EPS)
    return (y * g.astype(jnp.float32)).astype(x.dtype)


def l2_normalize(x):
    xf = x.astype(jnp.float32)
    return xf * lax.rsqrt(jnp.sum(xf * xf, axis=-1, keepdims=True) + EPS)


def adaln(cvec, w, b):
    m = jax.nn.silu(cvec) @ w + b
    return jnp.split(m[:, None, :], 3, axis=-1)


def short_conv(x, w):
    return lax.conv_general_dilated(
        x, w[:, None, :].astype(x.dtype), window_strides=(1,),
        padding=[(CONV_K // 2, CONV_K // 2)],
        dimension_numbers=('NWC', 'WIO', 'NWC'), feature_group_count=x.shape[-1])


def _chunk(a, c):
    b, t = a.shape[:2]
    return jnp.swapaxes(a.reshape((b, t // c, c) + a.shape[2:]), 2, 3)


def _unchunk(o):
    n, b, h, c, d = o.shape
    return jnp.transpose(o, (1, 0, 3, 2, 4)).reshape(b, n * c, h, d)


def gated_delta_rule(q, k, v, beta, g, s0):
    f32 = jnp.float32
    dv = v.shape[-1]
    q, k, v, beta, g = (_chunk(a.astype(f32), DELTA_CHUNK) for a in (q, k, v, beta, g))
    q = q * DK_A ** -0.5
    G = jnp.cumsum(g, axis=-1)
    tri = jnp.tril(jnp.ones((DELTA_CHUNK, DELTA_CHUNK), bool))
    decay = jnp.exp(jnp.where(tri, G[..., :, None] - G[..., None, :], -jnp.inf))
    kb = k * beta[..., None]
    a_mat = jnp.einsum('bnhid,bnhjd->bnhij', kb, k) * decay
    rhs = jnp.concatenate([v * beta[..., None], kb * jnp.exp(G)[..., None]], axis=-1)
    sol = lax.linalg.triangular_solve(a_mat, rhs, left_side=True, lower=True, unit_diagonal=True)
    w_v, w_k = sol[..., :dv], sol[..., dv:]
    qk = jnp.einsum('bnhid,bnhjd->bnhij', q, k) * decay
    q_dec = q * jnp.exp(G)[..., None]
    k_dec = k * jnp.exp(G[..., -1:] - G)[..., None]
    g_last = jnp.exp(G[..., -1])
    xs = tuple(jnp.moveaxis(a, 1, 0) for a in (w_v, w_k, qk, q_dec, k_dec, g_last))

    def step(S, inp):
        wv, wk, qk_c, qd, kd, gl = inp
        u = wv - jnp.einsum('bhcd,bhde->bhce', wk, S)
        o = jnp.einsum('bhcd,bhde->bhce', qd, S) + jnp.einsum('bhij,bhje->bhie', qk_c, u)
        S = S * gl[..., None, None] + jnp.einsum('bhcd,bhce->bhde', kd, u)
        return S, o

    s_fin, o = lax.scan(step, s0.astype(f32), xs)
    return _unchunk(o), s_fin


def gla_chunked(q, k, v, log_a, s0):
    f32 = jnp.float32
    q, k, v, la = (_chunk(a.astype(f32), GLA_CHUNK) for a in (q, k, v, log_a))
    q = q * DK_C ** -0.5
    Bc = jnp.cumsum(la, axis=-2)
    tri = jnp.tril(jnp.ones((GLA_CHUNK, GLA_CHUNK), bool))
    rel = jnp.exp(jnp.where(tri[..., None], Bc[..., :, None, :] - Bc[..., None, :, :], -jnp.inf))
    attn = jnp.einsum('bnhid,bnhjd,bnhijd->bnhij', q, k, rel)
    o_intra = jnp.einsum('bnhij,bnhje->bnhie', attn, v)
    q_dec = q * jnp.exp(Bc)
    k_dec = k * jnp.exp(Bc[..., -1:, :] - Bc)
    g_last = jnp.exp(Bc[..., -1, :])
    xs = tuple(jnp.moveaxis(a, 1, 0) for a in (q_dec, k_dec, v, o_intra, g_last))

    def step(S, inp):
        qd, kd, vc, oi, gl = inp
        o = jnp.einsum('bhcd,bhde->bhce', qd, S) + oi
        S = S * gl[..., None] + jnp.einsum('bhcd,bhce->bhde', kd, vc)
        return S, o

    s_fin, o = lax.scan(step, s0.astype(f32), xs)
    return _unchunk(o), s_fin


def axial_rope_tables(n_tokens, dim):
    rows = n_tokens // GRID_W
    row = jnp.repeat(jnp.arange(rows), GRID_W).astype(jnp.float32)
    col = jnp.tile(jnp.arange(GRID_W), rows).astype(jnp.float32)
    quarter = dim // 4
    freqs = ROPE_BASE ** (-jnp.arange(quarter, dtype=jnp.float32) / quarter)
    ar = row[:, None] * freqs
    ac = col[:, None] * freqs
    ang = jnp.concatenate([ar, ar, ac, ac], axis=-1)
    return jnp.cos(ang), jnp.sin(ang)


def apply_axial_rope(x, cos, sin):
    x1, x2, x3, x4 = jnp.split(x, 4, axis=-1)
    rot = jnp.concatenate([-x2, x1, -x4, x3], axis=-1)
    cos = cos[None, :, None, None, :]
    sin = sin[None, :, None, None, :]
    return (x.astype(jnp.float32) * cos + rot.astype(jnp.float32) * sin).astype(x.dtype)


def diff_softmax_attention(q, k, v, lam):
    b, t, h = q.shape[:3]
    qb = jnp.moveaxis(q.reshape((b, t // Q_BLOCK, Q_BLOCK) + q.shape[2:]), 1, 0)
    vf = v.astype(jnp.float32)

    def block(qi):
        s = jnp.einsum('bqhmd,bkhmd->bhmqk', qi, k).astype(jnp.float32) * DQK_B ** -0.5
        p = jax.nn.softmax(s, axis=-1)
        w = p[:, :, 0] - lam * p[:, :, 1]
        return jnp.einsum('bhqk,bkhd->bqhd', w, vf)

    o = lax.map(block, qb)
    return jnp.moveaxis(o, 0, 1).reshape(b, t, h, v.shape[-1])


def even_mixer(h, w_in, conv_w, a_log, dt_bias, gdn_g, lam_p, diff_g, w_out, lam_init, ctx):
    f32 = jnp.float32
    b, t, _ = h.shape
    qa, ka, va, za, ba, aa, qb, kb, vb, zb = _split(h @ w_in, WIDTHS_EVEN)
    qkv = jax.nn.silu(short_conv(jnp.concatenate([qa, ka, va], axis=-1), conv_w))
    qa, ka, va = _split(qkv, (H_A * DK_A, H_A * DK_A, H_A * DV_A))
    qa = l2_normalize(qa.reshape(b, t, H_A, DK_A))
    ka = l2_normalize(ka.reshape(b, t, H_A, DK_A))
    va = va.reshape(b, t, H_A, DV_A)
    beta = jax.nn.sigmoid(ba.astype(f32)).reshape(b, t, 2, H_A)
    g = -jnp.exp(a_log.astype(f32)) * jax.nn.softplus(
        aa.astype(f32).reshape(b, t, 2, H_A) + dt_bias.astype(f32))
    if ctx is None:
        s_f0 = jnp.zeros((b, H_A, DK_A, DV_A), f32)
        s_b0 = s_f0
    else:
        s_f0, s_b0, k_ctx, v_ctx = ctx
    o_f, s_f = gated_delta_rule(qa, ka, va, beta[:, :, 0], g[:, :, 0], s_f0)
    o_r, s_b = gated_delta_rule(qa[:, ::-1], ka[:, ::-1], va[:, ::-1],
                                beta[:, ::-1, 1], g[:, ::-1, 1], s_b0)
    o_a = rms_norm((o_f + o_r[:, ::-1]).astype(h.dtype), gdn_g) * jax.nn.silu(za.reshape(b, t, H_A, DV_A))
    q = qb.reshape(b, t, H_B, 2, DQK_B)
    k = kb.reshape(b, t, H_B, 2, DQK_B)
    v = vb.reshape(b, t, H_B, DV_B)
    if ctx is None:
        k_all, v_all = k, v
    else:
        cos, sin = axial_rope_tables(t, DQK_B)
        q = apply_axial_rope(q, cos, sin)
        k_all = jnp.concatenate([apply_axial_rope(k, cos, sin),
                                 k_ctx.reshape(b, -1, H_B, 2, DQK_B)], axis=1)
        v_all = jnp.concatenate([v, v_ctx], axis=1)
    lp = lam_p.astype(f32)
    lam = jnp.exp(jnp.sum(lp[0] * lp[1])) - jnp.exp(jnp.sum(lp[2] * lp[3])) + lam_init
    o = diff_softmax_attention(q, k_all, v_all, lam)
    o_bb = rms_norm(o.astype(h.dtype), diff_g) * (1.0 - lam_init) * jax.nn.silu(zb.reshape(b, t, H_B, DV_B))
    out = jnp.concatenate([o_a.reshape(b, t, -1), o_bb.reshape(b, t, -1)], axis=-1) @ w_out
    state = None if ctx is not None else (s_f, s_b, k.reshape(b, t, H_B, 2 * DQK_B), v)
    return out, state


def odd_mixer(h, w_in, w_gate, b_gate, gla_g, w_out, ctx):
    f32 = jnp.float32
    b, t, _ = h.shape
    q, k, v, z, glr = _split(h @ w_in, WIDTHS_ODD)
    q = q.reshape(b, t, H_C, DK_C)
    k = k.reshape(b, t, H_C, DK_C)
    v = v.reshape(b, t, H_C, DV_C)
    gate = jnp.einsum('btrl,rlk->btrk', glr.reshape(b, t, 2, GATE_RANK).astype(f32),
                      w_gate.astype(f32)) + b_gate.astype(f32)
    log_a = (jax.nn.log_sigmoid(gate) / GLA_NORMALIZER).reshape(b, t, 2, H_C, DK_C)
    if ctx is None:
        s_f0 = jnp.zeros((b, H_C, DK_C, DV_C), f32)
        s_b0 = s_f0
    else:
        s_f0, s_b0 = ctx
    o_f, s_f = gla_chunked(q, k, v, log_a[:, :, 0], s_f0)
    o_r, s_b = gla_chunked(q[:, ::-1], k[:, ::-1], v[:, ::-1], log_a[:, ::-1, 1], s_b0)
    o = rms_norm((o_f + o_r[:, ::-1]).astype(h.dtype), gla_g) * jax.nn.silu(z.reshape(b, t, H_C, DV_C))
    out = o.reshape(b, t, -1) @ w_out
    state = None if ctx is not None else (s_f, s_b)
    return out, state


def setup_inputs(seed: int = 0) -> dict:
    key = jax.random.key(seed)
    ks = jax.random.split(key, 26)
    f32 = jnp.float32
    D = D_MODEL

    def nrm(k, shape, scale):
        return jax.random.normal(k, shape, f32) * scale

    def gain(k, shape):
        return 1.0 + 0.02 * jax.random.normal(k, shape, f32)

    dt = jnp.exp(jax.random.uniform(ks[13], (N_EVEN, 2, H_A), f32, math.log(1e-3), math.log(1e-1)))
    return {
        'x_prompt': nrm(ks[0], (BATCH, SEQ, D), 1.0),
        'x_sample': nrm(ks[1], (DEC_BATCH, DEC_SEQ, D), 1.0),
        'c': nrm(ks[2], (DEC_BATCH, D), 1.0),
        'state_gdn': nrm(ks[3], (DEC_BATCH, N_EVEN, 2, H_A, DK_A, DV_A), 0.1),
        'cache_k': nrm(ks[4], (DEC_BATCH, N_EVEN, PAST_LEN, H_B, 2 * DQK_B), 1.0),
        'cache_v': nrm(ks[5], (DEC_BATCH, N_EVEN, PAST_LEN, H_B, DV_B), 1.0),
        'state_gla': nrm(ks[6], (DEC_BATCH, N_ODD, 2, H_C, DK_C, DV_C), 1.0),
        'c_ctx': nrm(ks[7], (D,), 1.0),
        'w_ada': nrm(ks[8], (DEPTH, D, 3 * D), D ** -0.5),
        'b_ada': nrm(ks[9], (DEPTH, 3 * D), 0.01),
        'norm_pre': gain(ks[10], (DEPTH, D)),
        'norm_post': gain(ks[11], (DEPTH, D)),
        'w_in_even': nrm(ks[12], (N_EVEN, D, D_IN_EVEN), D ** -0.5),
        'conv_even': nrm(ks[14], (N_EVEN, CONV_K, 2 * H_A * DK_A + H_A * DV_A), CONV_K ** -0.5),
        'a_log_even': jnp.log(jax.random.uniform(ks[15], (N_EVEN, 2, H_A), f32, 1.0, 16.0)),
        'dt_bias_even': dt + jnp.log(-jnp.expm1(-dt)),
        'gdn_norm_even': gain(ks[16], (N_EVEN, DV_A)),
        'lam_even': nrm(ks[17], (N_EVEN, 4, DQK_B), 0.1),
        'diff_norm_even': gain(ks[18], (N_EVEN, DV_B)),
        'w_out_even': nrm(ks[19], (N_EVEN, MIX_EVEN, D), MIX_EVEN ** -0.5),
        'w_in_odd': nrm(ks[20], (N_ODD, D, D_IN_ODD), D ** -0.5),
        'w_gate_odd': nrm(ks[21], (N_ODD, 2, GATE_RANK, H_C * DK_C), GATE_RANK ** -0.5),
        'b_gate_odd': nrm(ks[22], (N_ODD, 2, H_C * DK_C), 0.1),
        'gla_norm_odd': gain(ks[23], (N_ODD, DV_C)),
        'w_out_odd': nrm(ks[24], (N_ODD, MIX_ODD, D), MIX_ODD ** -0.5),
    }


def reference(x_prompt, x_sample, c, state_gdn, cache_k, cache_v, state_gla, c_ctx,
              w_ada, b_ada, norm_pre, norm_post, w_in_even, conv_even, a_log_even,
              dt_bias_even, gdn_norm_even, lam_even, diff_norm_even, w_out_even,
              w_in_odd, w_gate_odd, b_gate_odd, gla_norm_odd, w_out_odd):
    y_p, y_s = x_prompt, x_sample
    gdn_states, ctx_keys, ctx_vals, gla_states = [], [], [], []
    for layer in range(DEPTH):
        sh_p, sc_p, gt_p = adaln(c_ctx[None, :], w_ada[layer], b_ada[layer])
        sh_s, sc_s, gt_s = adaln(c, w_ada[layer], b_ada[layer])
        h_p = rms_norm(y_p, norm_pre[layer]) * (1.0 + sc_p) + sh_p
        h_s = rms_norm(y_s, norm_pre[layer]) * (1.0 + sc_s) + sh_s
        if layer % 2 == 0:
            e = layer // 2
            lam_init = 0.8 - 0.6 * math.exp(-0.3 * layer)
            prm = (w_in_even[e], conv_even[e], a_log_even[e], dt_bias_even[e], gdn_norm_even[e],
                   lam_even[e], diff_norm_even[e], w_out_even[e], lam_init)
            o_p, (s_f, s_b, k_c, v_c) = even_mixer(h_p, *prm, None)
            gdn_states.append(jnp.stack([s_f, s_b], axis=1).astype(x_prompt.dtype))
            ctx_keys.append(k_c)
            ctx_vals.append(v_c)
            o_s, _ = even_mixer(h_s, *prm, (state_gdn[:, e, 0], state_gdn[:, e, 1],
                                            cache_k[:, e], cache_v[:, e]))
        else:
            od = layer // 2
            prm = (w_in_odd[od], w_gate_odd[od], b_gate_odd[od], gla_norm_odd[od], w_out_odd[od])
            o_p, (s_f, s_b) = odd_mixer(h_p, *prm, None)
            gla_states.append(jnp.stack([s_f, s_b], axis=1).astype(x_prompt.dtype))
            o_s, _ = odd_mixer(h_s, *prm, (state_gla[:, od, 0], state_gla[:, od, 1]))
        y_p = y_p + gt_p * rms_norm(o_p, norm_post[layer])
        y_s = y_s + gt_s * rms_norm(o_s, norm_post[layer])
    new_state_gdn = jnp.stack(gdn_states, axis=1)
    new_cache_k = jnp.stack(ctx_keys, axis=1)
    new_cache_v = jnp.stack(ctx_vals, axis=1)
    new_state_gla = jnp.stack(gla_states, axis=1)
    return (y_p, y_s, new_state_gdn, new_cache_k, new_cache_v, new_state_gla)
```

```python
import math
import numpy as np
import concourse.bass as bass
import concourse.mybir as mybir
from concourse.bass_utils import run_bass_kernel_spmd
from contextlib import ExitStack

F32 = mybir.dt.float32
BF16 = mybir.dt.bfloat16
AF = mybir.ActivationFunctionType
ALU = mybir.AluOpType
AX = mybir.AxisListType

ENGS = ("pe", "act", "dve", "pool", "sp")
N_DMA_SLOTS = 12
EPS = 1e-6
NEG = -30000.0


class Tok:
    __slots__ = ("w", "r", "name")

    def __init__(self, name=""):
        self.w = None
        self.r = {}
        self.name = name


class Prog:
    def __init__(self, nc):
        self.nc = nc
        self.es = ExitStack()
        self.q = {e: [] for e in ENGS}
        self.dmas = []
        self.n_sb = 0
        self.stack = [self.es]
        self.phase_dmas = []

    def sb(self, shape, dt=F32, name=None):
        self.n_sb += 1
        return self.stack[-1].enter_context(self.nc.sbuf_tensor(f"sb{self.n_sb}" + (name or ""), list(shape), dt))

    def ps(self, shape, dt=F32, name=None):
        self.n_sb += 1
        return self.es.enter_context(self.nc.psum_tensor(name or f"ps{self.n_sb}", list(shape), dt))

    def _deps(self, reads, writes):
        deps = set()
        for t in reads:
            if t.w is not None:
                deps.add(t.w)
        for t in writes:
            if t.w is not None:
                deps.add(t.w)
            deps.update(t.r.values())
        return deps

    def _commit(self, ref, key, reads, writes):
        for t in reads:
            t.r[key] = ref
        for t in writes:
            t.w = ref
            t.r = {}

    def op(self, eng, fn, reads=(), writes=()):
        deps = self._deps(reads, writes)
        ref = (eng, len(self.q[eng]))
        self.q[eng].append(dict(fn=fn, deps=deps, dma=None, marked=False))
        self._commit(ref, eng, reads, writes)
        return ref

    def dma(self, qeng, out, in_, reads=(), writes=(), **kw):
        deps = self._deps(reads, writes)
        did = len(self.dmas)
        ref = ("dma", did)
        self.dmas.append(dict(q=qeng))
        fn = lambda e, out=out, in_=in_, kw=kw: e.dma_start(out=out, in_=in_, **kw)
        self.q[qeng].append(dict(fn=fn, deps=deps, dma=did, marked=False))
        self._commit(ref, ("dma", did), reads, writes)
        self.phase_dmas.append(ref)
        return ref

    def barrier(self):
        refs = [(e, len(self.q[e]) - 1) for e in ENGS if len(self.q[e]) > 0 and self.q[e][-1]["fn"] is not None]
        refs = [r for r in refs if self.q[r[0]][r[1]]["dma"] is None] + list(self.phase_dmas)
        cr = []
        for e in ENGS:
            for i in range(len(self.q[e]) - 1, -1, -1):
                o = self.q[e][i]
                if o["fn"] is not None and o["dma"] is None:
                    cr.append((e, i))
                    break
        refs = cr + list(self.phase_dmas)
        for e in ENGS:
            self.wait(e, refs)
        self.phase_dmas = []

    def phase(self):
        prog = self

        class _Ph:
            def __enter__(s2):
                st = ExitStack()
                prog.stack.append(st)
                return st

            def __exit__(s2, *a):
                prog.barrier()
                prog.stack.pop().close()
                return False

        return _Ph()

    def wait(self, eng, refs):
        self.q[eng].append(dict(fn=None, deps=set(refs), dma=None, marked=False))

    def emit(self):
        nc = self.nc
        q = self.q
        for e in ENGS:
            for o in q[e]:
                for d in o["deps"]:
                    if d[0] != "dma":
                        q[d[0]][d[1]]["marked"] = True
        for e in ENGS:
            c = 0
            for o in q[e]:
                if o["marked"]:
                    c += 1
                o["cnt"] = c
        sems = {e: self.es.enter_context(nc.semaphore("s_" + e)) for e in ENGS}
        dsem, slot_val, slot_rr = {}, {}, {}
        for e in ENGS:
            if any(o["dma"] is not None for o in q[e]):
                dsem[e] = [self.es.enter_context(nc.semaphore(f"d_{e}{i}")) for i in range(N_DMA_SLOTS)]
                slot_val[e] = [0] * N_DMA_SLOTS
                slot_rr[e] = 0
        for e in ENGS:
            for o in q[e]:
                if o["dma"] is not None:
                    s = slot_rr[e]
                    slot_rr[e] = (s + 1) % N_DMA_SLOTS
                    d = self.dmas[o["dma"]]
                    d["slot"] = s
                    d["prev"] = slot_val[e][s]
                    slot_val[e][s] += 16
                    d["val"] = slot_val[e][s]
        stats = {e: [0, 0] for e in ENGS}

        def run(e, eng):
            waited = {}

            def need(key, sem, val):
                if waited.get(key, 0) >= val:
                    return
                waited[key] = val
                eng.wait_ge(sem, val)
                stats[e][1] += 1

            for o in q[e]:
                cw = {}
                for d in o["deps"]:
                    if d[0] == "dma":
                        dd = self.dmas[d[1]]
                        need(("d", dd["q"], dd["slot"]), dsem[dd["q"]][dd["slot"]], dd["val"])
                    else:
                        if d[0] == e and e == "pe":
                            continue
                        v = q[d[0]][d[1]]["cnt"]
                        if cw.get(d[0], 0) < v:
                            cw[d[0]] = v
                for e2, v in cw.items():
                    need(("c", e2), sems[e2], v)
                if o["dma"] is not None:
                    dd = self.dmas[o["dma"]]
                    if dd["prev"] > 0:
                        need(("d", e, dd["slot"]), dsem[e][dd["slot"]], dd["prev"])
                if o["fn"] is None:
                    continue
                ins = o["fn"](eng)
                stats[e][0] += 1
                if o["dma"] is not None:
                    dd = self.dmas[o["dma"]]
                    ins.then_inc(dsem[e][dd["slot"]], 16)
                elif o["marked"]:
                    ins.then_inc(sems[e], 1)

        with nc.Block() as block:
            @block.sync
            def _(eng):
                run("sp", eng)

            @block.scalar
            def _(eng):
                run("act", eng)

            @block.vector
            def _(eng):
                run("dve", eng)

            @block.gpsimd
            def _(eng):
                run("pool", eng)

            @block.tensor
            def _(eng):
                run("pe", eng)
        self.stats = stats
        self.es.close()


class B:
    def __init__(self, P, shape, dt=F32, ps=False, name=None):
        self.t = P.ps(shape, dt, name) if ps else P.sb(shape, dt, name)
        self.k = Tok(name or "")

    def __getitem__(self, idx):
        return self.t[idx]


D = 2048
KT = 16
NTT = 12
SEQS = [(0, 2, 0), (2, 2, 0), (4, 8, 1)]
DEPTH = 4


def host_consts():
    c = {}
    c["k_ident"] = np.eye(128, dtype=np.float32)
    k = np.arange(128)
    tri = (k[:, None] <= k[None, :]).astype(np.float32)
    c["k_tri"] = np.stack([tri, tri.T.copy()], axis=1).astype(np.float32)
    c["k_ones"] = np.ones((128, 128), np.float32)
    j = k[:, None]
    i = k[None, :]
    negm = np.zeros((128, 2, 2, 128), np.float32)
    negm[:, 0, 0] = np.where(i >= j, 0.0, NEG)
    negm[:, 0, 1] = np.where(i > j, 0.0, NEG)
    negm[:, 1, 0] = np.where(i <= j, 0.0, NEG)
    negm[:, 1, 1] = np.where(i < j, 0.0, NEG)
    c["k_negm"] = negm.reshape(128, 2, 256)
    sel = np.zeros((16, 16, 128), np.float32)
    for r in range(16):
        sel[r, r, :] = 1.0
    c["k_sel"] = sel
    m01 = np.zeros((128, 2, 128), np.float32)
    m01[:, 0] = (i >= j)
    m01[:, 1] = (i <= j)
    c["k_m01"] = m01
    bm = np.zeros((128, 5, 128), np.float32)
    bm[:, 0] = (j // 8 == i // 8)
    for li, s_ in enumerate((8, 16, 32, 64)):
        bm[:, 1 + li] = (j // (2 * s_) == i // (2 * s_)) & (j // s_ != i // s_)
    c["k_bm"] = bm
    t = np.arange(1024)
    row = (t // 64).astype(np.float32)
    col = (t % 64).astype(np.float32)
    freqs = (10000.0 ** (-np.arange(16, dtype=np.float32) / 16)).astype(np.float32)
    ar = row[:, None] * freqs
    ac = col[:, None] * freqs
    ang = np.concatenate([ar, ar, ac, ac], axis=-1).astype(np.float32)
    cos = np.cos(ang).astype(np.float32)
    sin = np.sin(ang).astype(np.float32)
    ssin = sin.copy()
    ssin[:, 0:16] *= -1.0
    ssin[:, 32:48] *= -1.0
    c["k_rope"] = np.stack([cos, ssin], axis=1).astype(np.float32)
    return c


IN_SPECS = [
    ("xp", [512, D]), ("xs", [1024, D]), ("cT", [128, 16, 2]),
    ("sgdn", [2, 2, 8, 128, 128]), ("ck", [2, 256, 8, 128]), ("cv", [2, 256, 8, 128]),
    ("sgla", [2, 2, 4, 256, 512]),
    ("w_ada", [4, D, 6144]), ("b_adaT", [4, 128, 48]), ("b_ada", [4, 6144]),
    ("npreT", [4, 128, 16]), ("npost", [4, D]),
    ("w_in_even", [2, D, 8224]), ("convT", [2, 128, 24, 5]), ("a_log", [2, 16]), ("dt_bias", [2, 16]),
    ("gdn_norm", [2, 128]), ("lam", [2, 256]), ("diff_norm", [2, 128]), ("w_out_even", [2, D, D]),
    ("w_in_odd", [2, D, 6176]), ("w_gate", [2, 2, 16, 1024]), ("b_gate", [2, 2, 1024]),
    ("gla_norm", [2, 512]), ("w_out_odd", [2, D, D]),
    ("k_ident", [128, 128]), ("k_tri", [128, 2, 128]), ("k_ones", [128, 128]), ("k_negm", [128, 2, 256]),
    ("k_sel", [16, 16, 128]), ("k_m01", [128, 2, 128]), ("k_rope", [1024, 2, 64]), ("k_bm", [128, 5, 128]),
]
OUT_SPECS = [
    ("yp", [512, D]), ("ys", [1024, D]), ("o_gdn", [2, 2, 2, 8, 128, 128]),
    ("o_ck", [2, 2, 256, 8, 128]), ("o_cv", [2, 2, 256, 8, 128]), ("o_gla", [2, 2, 2, 4, 256, 512]),
]


def build(n_layers=DEPTH, dbg=False):
    nc = bass.Bass("TRN2", target_bir_lowering=False)
    I = {n: nc.dram_tensor(n, s, F32, kind="ExternalInput").ap() for n, s in IN_SPECS}
    O = {n: nc.dram_tensor(n, s, F32, kind="ExternalOutput").ap() for n, s in OUT_SPECS}
    ybuf = nc.dram_tensor("ybuf", [2, 1536, D], F32, kind="Internal").ap()
    obuf = nc.dram_tensor("obuf", [1536, D], F32, kind="Internal").ap()
    if dbg:
        O["dbg_y0"] = nc.dram_tensor("dbg_y0", [1536, D], F32, kind="ExternalOutput").ap()
        for l_ in range(min(2, n_layers)):
            O[f"dbg_hT{l_}"] = nc.dram_tensor(f"dbg_hT{l_}", [128, KT, 1536], BF16, kind="ExternalOutput").ap()
            O[f"dbg_oT{l_}"] = nc.dram_tensor(f"dbg_oT{l_}", [128, KT, 1536], BF16, kind="ExternalOutput").ap()
    P = Prog(nc)
    out_refs = []

    def mm(out, lhsT, rhs, start, stop, R, W):
        P.op("pe", lambda e: e.matmul(out, lhsT, rhs, start=start, stop=stop), reads=R, writes=W)

    def tr(out, in_, ident, R, W):
        P.op("pe", lambda e: e.transpose(out, in_, ident), reads=R, writes=W)

    def act(out, in_, func, R, W, **kw):
        P.op("act", lambda e: e.activation(out=out, in_=in_, func=func, **kw), reads=R, writes=W)

    def ts(eng, out, in0, s1, s2, op0, op1, R, W):
        if s2 is None:
            P.op(eng, lambda e: e.tensor_scalar(out, in0, s1, None, op0), reads=R, writes=W)
        else:
            P.op(eng, lambda e: e.tensor_scalar(out, in0, s1, s2, op0, op1), reads=R, writes=W)

    def tt(eng, out, in0, in1, op, R, W):
        P.op(eng, lambda e: e.tensor_tensor(out, in0, in1, op), reads=R, writes=W)

    def stt(eng, out, in0, sc_, in1, op0, op1, R, W):
        P.op(eng, lambda e: e.scalar_tensor_tensor(out, in0, sc_, in1, op0, op1), reads=R, writes=W)

    def cp(eng, out, in_, R, W):
        if eng == "act":
            P.op("act", lambda e: e.copy(out, in_), reads=R, writes=W)
        else:
            P.op(eng, lambda e: e.tensor_copy(out, in_), reads=R, writes=W)

    def red(eng, out, in_, R, W):
        P.op(eng, lambda e: e.reduce_sum(out, in_, AX.X), reads=R, writes=W)

    def rsq(out, tmp, in_, scale, W):
        act(tmp, in_, AF.Ln, [], W, scale=scale, bias=epsb[:, 0:1])
        act(out, tmp, AF.Exp, [], W, scale=-0.5)

    def mset(eng, ap, v, W):
        P.op(eng, lambda e: e.memset(ap, v), writes=W)

    identf = B(P, [128, 128]); identb = B(P, [128, 128], BF16)
    trif = B(P, [128, 2, 128]); onesf = B(P, [128, 128]); onesb = B(P, [128, 128], BF16)
    negmb = B(P, [128, 2, 256], BF16); self_ = B(P, [16, 16, 128]); m01 = B(P, [128, 2, 128])
    P.dma("sp", identf[:], I["k_ident"], writes=[identf.k])
    P.dma("pool", identb[:], I["k_ident"], writes=[identb.k])
    P.dma("sp", trif[:], I["k_tri"], writes=[trif.k])
    P.dma("sp", onesf[:], I["k_ones"], writes=[onesf.k])
    P.dma("pool", onesb[:], I["k_ones"], writes=[onesb.k])
    P.dma("pool", negmb[:], I["k_negm"], writes=[negmb.k])
    P.dma("sp", self_[:], I["k_sel"], writes=[self_.k])
    P.dma("sp", m01[:], I["k_m01"], writes=[m01.k])

    hT = B(P, [128, KT, 1536], BF16, name="hT")
    oT = B(P, [128, KT, 1536], BF16, name="oT")
    WB = [B(P, [128, KT, 512], BF16, name=f"wb{i}") for i in range(2)]
    wb_i = [0]
    psP = [B(P, [128, 512], ps=True, name=f"psP{i}") for i in range(2)]
    psT = [B(P, [128, 1024], BF16, ps=True, name=f"psT{i}") for i in range(2)]
    psX = B(P, [128, 512], ps=True, name="psX")
    psE = B(P, [128, 512], ps=True, name="psE")
    psC = [B(P, [128, 512], ps=True, name=f"psC{i}") for i in range(2)]
    psP_i = [0]

    def next_wb():
        b = WB[wb_i[0] % 2]
        wb_i[0] += 1
        return b

    def load_w(dst, src, ncols, col0=0):
        v = src.rearrange("(kt p) c -> p kt c", p=128)
        for q4 in range(4):
            P.dma("pool", dst.t[:, q4 * 4:(q4 + 1) * 4, col0:col0 + ncols], v[:, q4 * 4:(q4 + 1) * 4, :], writes=[dst.k])

    def next_psP():
        b = psP[psP_i[0] % 2]
        psP_i[0] += 1
        return b

    sc = B(P, [128, 16, 2], BF16); cTf = B(P, [128, 16, 2])
    P.dma("sp", cTf[:], I["cT"], writes=[cTf.k])
    act(sc[:], cTf[:], AF.Silu, [cTf.k], [sc.k])
    modT = B(P, [128, 32, 2]); bT = B(P, [128, 48]); npreT = B(P, [128, 16])
    gsc = B(P, [128, 16, 2])
    sm = B(P, [128, 16]); rs = B(P, [128, 16])
    epsb = B(P, [128, 1])
    mset("pool", epsb[:], EPS, [epsb.k])
    obuf_k = [Tok() for _ in range(NTT)]
    ybuf_k = [[Tok() for _ in range(NTT)] for _ in range(2)]

    def adaln(l):
        P.dma("sp", bT[:], I["b_adaT"][l], writes=[bT.k])
        P.dma("sp", npreT[:], I["npreT"][l], writes=[npreT.k])
        for blk in range(8):
            wb = next_wb()
            load_w(wb, I["w_ada"][l][:, blk * 512:(blk + 1) * 512], 512)
            for j in range(4):
                for kt in range(KT):
                    mm(psX[:, j * 2:j * 2 + 2], wb[:, kt, j * 128:(j + 1) * 128], sc[:, kt, :], kt == 0, kt == KT - 1,
                       [wb.k, sc.k], [psX.k])
            tt("dve", modT[:, blk * 4:(blk + 1) * 4, :], psX[:, 0:8].rearrange("p (a b) -> p a b", b=2),
               bT[:, blk * 4:(blk + 1) * 4].unsqueeze(2).to_broadcast([128, 4, 2]), ALU.add, [bT.k], [psX.k, modT.k])
        ts("dve", gsc[:], modT[:, 16:32, :], 1.0, None, ALU.add, None, [modT.k], [gsc.k])
        tt("dve", gsc[:], gsc[:], npreT[:].unsqueeze(2).to_broadcast([128, 16, 2]), ALU.mult, [npreT.k], [gsc.k])

    def ysrc(l, tt_):
        if l == 0:
            return I["xp"][tt_ * 128:(tt_ + 1) * 128, :] if tt_ < 4 else I["xs"][(tt_ - 4) * 128:(tt_ - 3) * 128, :]
        return ybuf[(l - 1) % 2, tt_ * 128:(tt_ + 1) * 128, :]

    def ydst(l, tt_):
        if l == n_layers - 1:
            return O["yp"][tt_ * 128:(tt_ + 1) * 128, :] if tt_ < 4 else O["ys"][(tt_ - 4) * 128:(tt_ - 3) * 128, :]
        return ybuf[l % 2, tt_ * 128:(tt_ + 1) * 128, :]

    def build_hT(l):
        with P.phase():
            yt = [B(P, [128, D]) for i in range(2)]
            xh = B(P, [128, D], BF16)
            for tt_ in range(NTT):
                g = 0 if tt_ < 4 else 1
                y = yt[tt_ % 2]
                P.dma("sp", y[:], ysrc(l, tt_), reads=[ybuf_k[(l - 1) % 2][tt_]] if l > 0 else [], writes=[y.k])
                act(xh[:], y[:], AF.Square, [y.k], [xh.k, sm.k], accum_out=sm[:, 0:1])
                rsq(sm[:, 2:3], sm[:, 1:2], sm[:, 0:1], 1.0 / D, [sm.k, epsb.k])
                ts("dve", xh[:], y[:], sm[:, 2:3], None, ALU.mult, None, [y.k, sm.k], [xh.k])
                for half in range(2):
                    pt = psT[half]
                    for j in range(8):
                        kt = half * 8 + j
                        tr(pt[:, j * 128:(j + 1) * 128], xh[:, kt * 128:(kt + 1) * 128], identb[:], [xh.k, identb.k], [pt.k])
                    for j in range(8):
                        kt = half * 8 + j
                        dst = hT[:, kt, tt_ * 128:(tt_ + 1) * 128]
                        if half == 0:
                            act(dst, pt[:, j * 128:(j + 1) * 128], AF.Identity, [gsc.k, modT.k], [pt.k, hT.k],
                                scale=gsc[:, kt, g:g + 1], bias=modT[:, kt, g:g + 1])
                        else:
                            ts("dve", dst, pt[:, j * 128:(j + 1) * 128], gsc[:, kt, g:g + 1], modT[:, kt, g:g + 1], ALU.mult, ALU.add,
                               [gsc.k, modT.k], [pt.k, hT.k])

    def out_proj(l, w_out):
        with P.phase():
            ob = [B(P, [128, 512]) for i in range(2)]
            scb = B(P, [128, 32, 128], BF16)
            gpost = [B(P, [128, D]) for g in range(2)]
            bgb = B(P, [128, 512]); npb = B(P, [128, 512])
            yt = [B(P, [128, D]) for i in range(2)]
            xh = B(P, [128, D], BF16)
            cp("dve", scb[:], sc[:].rearrange("p a b -> p (a b)").unsqueeze(2).to_broadcast([128, 32, 128]), [sc.k], [scb.k])
            for blk in range(4):
                wb = next_wb()
                load_w(wb, I["w_ada"][l][:, 4096 + blk * 512:4096 + (blk + 1) * 512], 512)
                cs = slice(blk * 512, (blk + 1) * 512)
                P.dma("sp", bgb[:], I["b_ada"][l, 4096 + blk * 512:4096 + (blk + 1) * 512].partition_broadcast(128), writes=[bgb.k])
                P.dma("sp", npb[:], I["npost"][l, cs].partition_broadcast(128), writes=[npb.k])
                for g in range(2):
                    pp = next_psP()
                    for kt in range(KT):
                        mm(pp[:], scb[:, kt * 2 + g, :], wb[:, kt, :], kt == 0, kt == KT - 1, [scb.k, wb.k], [pp.k])
                    tt("dve", gpost[g][:, cs], pp[:], bgb[:], ALU.add, [bgb.k], [pp.k, gpost[g].k])
                    tt("pool", gpost[g][:, cs], gpost[g][:, cs], npb[:], ALU.mult, [npb.k], [gpost[g].k])
            oi = 0
            for blk in range(4):
                wb = next_wb()
                load_w(wb, w_out[:, blk * 512:(blk + 1) * 512], 512)
                for tt_ in range(NTT):
                    pp = next_psP()
                    for kt in range(KT):
                        mm(pp[:], oT[:, kt, tt_ * 128:(tt_ + 1) * 128], wb[:, kt, :], kt == 0, kt == KT - 1, [oT.k, wb.k], [pp.k])
                    o = ob[oi % 2]
                    oi += 1
                    cp("act", o[:], pp[:], [], [pp.k, o.k])
                    P.dma("sp", obuf[tt_ * 128:(tt_ + 1) * 128, blk * 512:(blk + 1) * 512], o[:], reads=[o.k], writes=[obuf_k[tt_]])
            for tt_ in range(NTT):
                g = 0 if tt_ < 4 else 1
                o = yt[0]
                y = yt[1]
                P.dma("sp", o[:], obuf[tt_ * 128:(tt_ + 1) * 128, :], reads=[obuf_k[tt_]], writes=[o.k])
                P.dma("sp", y[:], ysrc(l, tt_), reads=[ybuf_k[(l - 1) % 2][tt_]] if l > 0 else [], writes=[y.k])
                act(xh[:], o[:], AF.Square, [o.k], [xh.k, sm.k], accum_out=sm[:, 0:1])
                rsq(sm[:, 2:3], sm[:, 1:2], sm[:, 0:1], 1.0 / D, [sm.k, epsb.k])
                stt("dve", o[:], o[:], sm[:, 2:3], gpost[g][:], ALU.mult, ALU.mult, [sm.k, gpost[g].k], [o.k])
                tt("pool", y[:], y[:], o[:], ALU.add, [o.k], [y.k])
                last = (l == n_layers - 1)
                r = P.dma("sp", ydst(l, tt_), y[:], reads=[y.k], writes=[] if last else [ybuf_k[l % 2][tt_]])
                if last:
                    out_refs.append(r)
                if dbg and l == 0:
                    out_refs.append(P.dma("sp", O["dbg_y0"][tt_ * 128:(tt_ + 1) * 128, :], y[:], reads=[y.k]))

    def interleave(gens, offsets=None):
        gens = list(gens)
        offs = {id(g_): (offsets[i] if offsets else 0) for i, g_ in enumerate(gens)}
        rnd = 0
        while gens:
            for g_ in list(gens):
                if offs[id(g_)] > rnd:
                    continue
                try:
                    next(g_)
                except StopIteration:
                    gens.remove(g_)
            rnd += 1

    class Ctx:
        pass

    def even_layer(l):
      with P.phase():
        e = l // 2
        W = I["w_in_even"][e]
        lam_init = 0.8 - 0.6 * math.exp(-0.3 * l)
        beta = B(P, [128, 1, 16]); nbeta = B(P, [128, NTT, 16]); gg = B(P, [128, 1, 16])
        Gc = B(P, [128, 1, 16]); nG = B(P, [128, NTT, 16])
        eG = B(P, [128, NTT, 16]); kds = B(P, [128, NTT, 16]); gl = B(P, [128, NTT, 16])
        GT = B(P, [16, NTT, 128])
        wba = B(P, [128, KT, 32], BF16)
        alog_b = B(P, [128, 16]); dtb_b = B(P, [128, 16]); nea = B(P, [128, 16])
        convw = B(P, [128, 24, 5])
        gdn_gb = B(P, [128, 128]); diff_gb = B(P, [128, 128]); lam_b = B(P, [128, 256]); lamv = B(P, [128, 4])
        dg = B(P, [128, 15, 128], BF16)
        bmk = B(P, [128, 5, 128], BF16)
        P.dma("pool", bmk[:], I["k_bm"], writes=[bmk.k])
        Sf = [B(P, [128, 128]) for i in range(2)]
        Sb = [B(P, [128, 128], BF16) for i in range(2)]
        ofin = B(P, [128, 128]); ofb = B(P, [128, 128], BF16)

        load_w(wba, W[:, 4096:4128], 32)
        P.dma("sp", alog_b[:], I["a_log"][e].partition_broadcast(128), writes=[alog_b.k])
        P.dma("sp", dtb_b[:], I["dt_bias"][e].partition_broadcast(128), writes=[dtb_b.k])
        P.dma("sp", convw[:], I["convT"][e], writes=[convw.k])
        P.dma("sp", gdn_gb[:], I["gdn_norm"][e].partition_broadcast(128), writes=[gdn_gb.k])
        P.dma("sp", diff_gb[:], I["diff_norm"][e].partition_broadcast(128), writes=[diff_gb.k])
        P.dma("sp", lam_b[:], I["lam"][e].partition_broadcast(128), writes=[lam_b.k])
        act(nea[:], alog_b[:], AF.Exp, [alog_b.k], [nea.k])
        ts("dve", nea[:], nea[:], -1.0, None, ALU.mult, None, [], [nea.k])
        ts("dve", diff_gb[:], diff_gb[:], 1.0 - lam_init, None, ALU.mult, None, [], [diff_gb.k])
        tt("dve", lam_b[:, 0:64], lam_b[:, 0:64], lam_b[:, 64:128], ALU.mult, [], [lam_b.k])
        tt("dve", lam_b[:, 128:192], lam_b[:, 128:192], lam_b[:, 192:256], ALU.mult, [], [lam_b.k])
        red("dve", lamv[:, 0:1], lam_b[:, 0:64], [lam_b.k], [lamv.k])
        red("dve", lamv[:, 1:2], lam_b[:, 128:192], [lam_b.k], [lamv.k])
        act(lamv[:, 0:2], lamv[:, 0:2], AF.Exp, [], [lamv.k])
        tt("dve", lamv[:, 2:3], lamv[:, 1:2], lamv[:, 0:1], ALU.subtract, [], [lamv.k])
        ts("dve", lamv[:, 3:4], lamv[:, 2:3], -lam_init, None, ALU.add, None, [], [lamv.k])
        for tt_ in range(NTT):
            for kt in range(KT):
                mm(psX[:, 0:32], hT[:, kt, tt_ * 128:(tt_ + 1) * 128], wba[:, kt, :], kt == 0, kt == KT - 1, [hT.k, wba.k], [psX.k])
            act(beta[:, 0, :], psX[:, 0:16], AF.Sigmoid, [], [psX.k, beta.k])
            tt("dve", gg[:, 0, :], psX[:, 16:32], dtb_b[:], ALU.add, [dtb_b.k], [psX.k, gg.k])
            act(gg[:, 0, :], gg[:, 0, :], AF.Exp, [], [gg.k])
            act(gg[:, 0, :], gg[:, 0, :], AF.Ln, [], [gg.k], bias=1.0)
            tt("dve", gg[:, 0, :], gg[:, 0, :], nea[:], ALU.mult, [nea.k], [gg.k])
            ts("dve", nbeta[:, tt_, :], beta[:, 0, :], -1.0, None, ALU.mult, None, [beta.k], [nbeta.k])
            mm(psX[:, 32:40], trif[:, 0, :], gg[:, 0, 0:8], True, True, [trif.k, gg.k], [psX.k])
            mm(psX[:, 40:48], trif[:, 1, :], gg[:, 0, 8:16], True, True, [trif.k, gg.k], [psX.k])
            mm(psX[:, 48:64], onesf[:], gg[:, 0, :], True, True, [onesf.k, gg.k], [psX.k])
            cp("dve", Gc[:, 0, :], psX[:, 32:48], [], [psX.k, Gc.k])
            tt("dve", kds[:, tt_, :], psX[:, 48:64], Gc[:, 0, :], ALU.subtract, [Gc.k], [psX.k, kds.k])
            act(gl[:, tt_, :], psX[:, 48:64], AF.Exp, [], [psX.k, gl.k])
            act(kds[:, tt_, :], kds[:, tt_, :], AF.Exp, [], [kds.k])
            act(eG[:, tt_, :], Gc[:, 0, :], AF.Exp, [Gc.k], [eG.k])
            ts("dve", nG[:, tt_, :], Gc[:, 0, :], -1.0, None, ALU.mult, None, [Gc.k], [nG.k])
            tr(psX[0:16, 64:192], Gc[:, 0, :], identf[:], [Gc.k, identf.k], [psX.k])
            cp("dve", GT[:, tt_, :], psX[0:16, 64:192], [], [psX.k, GT.k])

        def finish_o(src, gain, zsrc, feat_kt, tok0, R):
            act(ofb[:], src, AF.Square, R, [ofb.k, rs.k], accum_out=rs[:, 0:1])
            rsq(rs[:, 2:3], rs[:, 1:2], rs[:, 0:1], 1.0 / 128, [rs.k, epsb.k])
            stt("dve", ofin[:], src, rs[:, 2:3], gain[:], ALU.mult, ALU.mult, R + [gain.k, rs.k], [ofin.k])
            tt("dve", ofb[:], ofin[:], zsrc, ALU.mult, R, [ofb.k, ofin.k])
            pt = psT[1]
            tr(pt[:, 0:128], ofb[:], identb[:], [ofb.k, identb.k], [pt.k])
            cp("act", oT[:, feat_kt, tok0:tok0 + 128], pt[:, 0:128], [], [pt.k, oT.k])

        def finish_seq(osrc, R, gain, zsrc, zk, feat_kt, t0_, n_, osq, ofbb):
            tt("dve", osq[:, 0:n_, :], osrc[:, 0:n_, :], osrc[:, 0:n_, :], ALU.mult, R, [osq.k])
            red("dve", rs[:, 4:4 + n_], osq[:, 0:n_, :], [osq.k], [rs.k])
            act(rs[:, 4:4 + n_], rs[:, 4:4 + n_], AF.Ln, [], [rs.k, epsb.k], scale=1.0 / 128, bias=epsb[:, 0:1])
            act(rs[:, 4:4 + n_], rs[:, 4:4 + n_], AF.Exp, [], [rs.k], scale=-0.5)
            tt("dve", osq[:, 0:n_, :], osrc[:, 0:n_, :], rs[:, 4:4 + n_].unsqueeze(2).to_broadcast([128, n_, 128]), ALU.mult,
               R + [rs.k], [osq.k])
            tt("dve", osq[:, 0:n_, :], osq[:, 0:n_, :], gain[:].unsqueeze(1).to_broadcast([128, n_, 128]), ALU.mult, [gain.k], [osq.k])
            tt("dve", ofbb[:, 0:n_, :], osq[:, 0:n_, :], zsrc[:, 0:n_, :], ALU.mult, [osq.k, zk], [ofbb.k])
            pt = psT[1]
            for lt_ in range(n_):
                tr(pt[:, lt_ * 128:(lt_ + 1) * 128], ofbb[:, lt_, :], identb[:], [ofbb.k, identb.k], [pt.k])
            cp("act", oT[:, feat_kt, t0_ * 128:(t0_ + n_) * 128], pt[:, 0:n_ * 128], [], [pt.k, oT.k])

        def gdn_unit(cx, qkv, kqT, rn, oacc, oacc_k, tk, tile_g, lt, h, d):
            c = d * 8 + h
            E, X, C, pt = cx.X, cx.X, cx.C, cx.T
            kq_k, qkv_k = tk
            Em, Nf, Ntf, Dt, NoT, T1b, Dm, NP, Nt = cx.Em, cx.Nf, cx.Ntf, cx.Dt, cx.NoT, cx.T1b, cx.Dm, cx.NP, cx.Nt
            qkTm, kdec, Rb, ub = cx.qkTm, cx.kdec, cx.Rb, cx.ub
            mm(E[:, 256:512], kqT[:, lt, 0:128], kqT[:, lt, 0:256], True, True, [kq_k[lt]], [E.k])
            for half in range(2):
                mm(X[:, half * 128:(half + 1) * 128], self_[:, c, :], GT[:, tile_g, :], True, False, [self_.k, GT.k], [X.k])
                mm(X[:, half * 128:(half + 1) * 128], identb[:], negmb[:, d, half * 128:(half + 1) * 128], False, True,
                   [identb.k, negmb.k], [X.k])
            yield
            act(Em[:], X[:, 0:256], AF.Exp, [nG.k], [X.k, Em.k], bias=nG[:, tile_g, c:c + 1])
            yield
            stt("dve", Nf[:], E[:, 256:384], nbeta[:, tile_g, c:c + 1], Em[:, 128:256], ALU.mult, ALU.mult,
                [nbeta.k, Em.k], [E.k, Nf.k])
            tt("dve", qkTm[:], E[:, 384:512], Em[:, 0:128], ALU.mult, [Em.k], [E.k, qkTm.k])
            ts("pool", kdec[:], qkv[:, lt, 128:256], rn[:, lt, 1:2], kds[:, tile_g, c:c + 1], ALU.mult, ALU.mult,
               [qkv_k[lt], rn.k, kds.k], [kdec.k])
            yield
            tr(pt[:, 0:128], Nf[:], identb[:], [Nf.k, identb.k], [pt.k])
            cp("act", Ntf[:], pt[:, 0:128], [], [pt.k, Ntf.k])
            tt("pool", NP[0][:, 0:128], Nf[:], bmk[:, 0, :], ALU.mult, [Nf.k, bmk.k], [NP[0].k])
            cp("pool", NP[0][:, 128:256], identb[:], [identb.k], [NP[0].k])
            yield
            tt("pool", Nt[0][:], Ntf[:], bmk[:, 0, :], ALU.mult, [Ntf.k, bmk.k], [Nt[0].k])
            yield
            cur = 0
            for k in range(3):
                np_c, nt_c = NP[cur], Nt[cur]
                np_n, nt_n = NP[1 - cur], Nt[1 - cur]
                if k < 2:
                    mm(E[:, 0:256], nt_c[:], np_c[:, 0:256], True, False, [nt_c.k, np_c.k], [E.k])
                    mm(E[:, 128:256], identb[:], np_c[:, 128:256], False, True, [identb.k, np_c.k], [E.k])
                    mm(X[:, 256:384], np_c[:, 0:128], nt_c[:], True, True, [np_c.k, nt_c.k], [X.k])
                    yield
                    cp("dve", np_n[:, 0:256], E[:, 0:256], [], [E.k, np_n.k])
                    cp("act", nt_n[:], X[:, 256:384], [], [X.k, nt_n.k])
                    yield
                else:
                    mm(E[:, 128:256], nt_c[:], np_c[:, 128:256], True, False, [nt_c.k, np_c.k], [E.k])
                    mm(E[:, 128:256], identb[:], np_c[:, 128:256], False, True, [identb.k, np_c.k], [E.k])
                    yield
                    cp("dve", Dm[0][:], E[:, 128:256], [], [E.k, Dm[0].k])
                    yield
                cur = 1 - cur
            dcur = 0
            for li in range(4):
                Dc, Dn = Dm[dcur], Dm[1 - dcur]
                tr(pt[:, 0:128], Dc[:], identb[:], [Dc.k, identb.k], [pt.k])
                cp("act", Dt[:], pt[:, 0:128], [], [pt.k, Dt.k])
                tt("pool", NoT[:], Ntf[:], bmk[:, 1 + li, :], ALU.mult, [Ntf.k, bmk.k], [NoT.k])
                yield
                mm(E[:, 0:128], NoT[:], Dc[:], True, True, [NoT.k, Dc.k], [E.k])
                yield
                cp("act", T1b[:], E[:, 0:128], [], [E.k, T1b.k])
                yield
                mm(E[:, 128:256], Dt[:], T1b[:], True, True, [Dt.k, T1b.k], [E.k])
                yield
                tt("dve", Dn[:], E[:, 128:256], Dc[:], ALU.add, [Dc.k], [E.k, Dn.k])
                yield
                dcur = 1 - dcur
            MT = Dm[dcur]
            S_f, S_b = Sf[d], Sb[d]
            mm(C[:, 0:128], kqT[:, lt, 0:128], S_b[:], True, True, [kq_k[lt], S_b.k], [C.k])
            mm(C[:, 128:256], kqT[:, lt, 128:256], S_b[:], True, True, [kq_k[lt], S_b.k], [C.k])
            yield
            stt("dve", Rb[:], C[:, 0:128], eG[:, tile_g, c:c + 1], qkv[:, lt, 256:384], ALU.mult, ALU.subtract,
                [eG.k, qkv_k[lt]], [C.k, Rb.k])
            stt("dve", oacc[:, lt, :], C[:, 128:256], eG[:, tile_g, c:c + 1], oacc[:, lt, :], ALU.mult, ALU.add,
                [eG.k], [C.k, oacc_k[lt]])
            yield
            mm(C[:, 256:384], MT[:], Rb[:], True, True, [MT.k, Rb.k], [C.k])
            yield
            act(ub[:], C[:, 256:384], AF.Copy, [nbeta.k], [C.k, ub.k], scale=nbeta[:, tile_g, c:c + 1])
            yield
            mm(C[:, 0:128], qkTm[:], ub[:], True, True, [qkTm.k, ub.k], [C.k])
            mm(C[:, 128:256], kdec[:], ub[:], True, True, [kdec.k, ub.k], [C.k])
            yield
            tt("dve", oacc[:, lt, :], C[:, 0:128], oacc[:, lt, :], ALU.add, [], [C.k, oacc_k[lt]])
            stt("dve", S_f[:], S_f[:], gl[:, tile_g, c:c + 1], C[:, 128:256], ALU.mult, ALU.add, [gl.k], [C.k, S_f.k])
            yield
            cp("act", S_b[:], S_f[:], [S_f.k], [S_b.k])
            yield

        def mk_gdn_ctx(X, E, C, T):
            cx = Ctx()
            cx.X, cx.E, cx.C, cx.T = X, E, C, T
            cx.Em = B(P, [128, 256])
            cx.Nf = B(P, [128, 128], BF16); cx.Ntf = B(P, [128, 128], BF16); cx.Dt = B(P, [128, 128], BF16)
            cx.NoT = B(P, [128, 128], BF16); cx.T1b = B(P, [128, 128], BF16)
            cx.Dm = [B(P, [128, 128], BF16) for i in range(2)]
            cx.NP = [B(P, [128, 256], BF16) for i in range(2)]
            cx.Nt = [B(P, [128, 128], BF16) for i in range(2)]
            cx.qkTm = B(P, [128, 128], BF16); cx.kdec = B(P, [128, 128], BF16)
            cx.Rb = B(P, [128, 128], BF16); cx.ub = B(P, [128, 128], BF16)
            return cx

        for h in range(8):
            wg = next_wb()
            for j in range(4):
                load_w(wg, W[:, j * 1024 + h * 128: j * 1024 + (h + 1) * 128], 128, col0=j * 128)
            wd = next_wb()
            for j in range(4):
                load_w(wd, W[:, 4128 + j * 1024 + h * 128: 4128 + j * 1024 + (h + 1) * 128], 128, col0=j * 128)
            for j in range(3):
                for tap in range(5):
                    ts("pool", dg[:, j * 5 + tap, :], identf[:], convw[:, j * 8 + h, tap:tap + 1], None, ALU.mult, None,
                       [identf.k, convw.k], [dg.k])
            for si, (t0, nt_, g) in enumerate(SEQS):
                T = nt_ * 128
                ctx = (g == 1)
                with P.phase():
                    xpad = B(P, [128, 3, 1028], BF16)
                    qkv = B(P, [128, 8, 384], BF16)
                    kqT = B(P, [128, 8, 256], BF16)
                    qkb = B(P, [128, 2, 128], BF16)
                    rn = B(P, [128, 8, 2])
                    zt = B(P, [128, 8, 128], BF16)
                    oacc = B(P, [128, 8, 128])
                    oacc_k = [Tok() for _ in range(8)]
                    osq = B(P, [128, 8, 128]); ofbb = B(P, [128, 8, 128], BF16)
                    cxs = [mk_gdn_ctx(psX, psX, psC[0], psT[0]), mk_gdn_ctx(psE, psE, psC[1], psT[1])]
                    kq_k = [Tok() for _ in range(8)]
                    qkv_k = [Tok() for _ in range(8)]
                    ready = set()
                    mset("pool", xpad[:, :, 0:2], 0.0, [xpad.k])
                    mset("pool", xpad[:, :, T + 2:T + 4], 0.0, [xpad.k])
                    mset("pool", oacc[:], 0.0, oacc_k)
                    for j in range(3):
                        for b0 in range(0, T, 512):
                            nb = min(512, T - b0)
                            pp = next_psP()
                            for kt in range(KT):
                                mm(pp[:, 0:nb], wg[:, kt, j * 128:(j + 1) * 128], hT[:, kt, t0 * 128 + b0:t0 * 128 + b0 + nb],
                                   kt == 0, kt == KT - 1, [wg.k, hT.k], [pp.k])
                            cp("act", xpad[:, j, 2 + b0:2 + b0 + nb], pp[:, 0:nb], [], [pp.k, xpad.k])
                    def prologue():
                        order = []
                        for i_ in range((nt_ + 1) // 2):
                            order.append(i_)
                            if nt_ - 1 - i_ != i_:
                                order.append(nt_ - 1 - i_)
                        for lt in order:
                            tg = t0 + lt
                            pp = next_psP()
                            for kt in range(KT):
                                mm(pp[:, 0:128], hT[:, kt, tg * 128:(tg + 1) * 128], wg[:, kt, 384:512], kt == 0, kt == KT - 1,
                                   [hT.k, wg.k], [pp.k])
                            act(zt[:, lt, :], pp[:, 0:128], AF.Silu, [], [pp.k, zt.k])
                            yield
                            pc_ = next_psP()
                            for j in range(3):
                                for tap in range(5):
                                    mm(pc_[:, j * 128:(j + 1) * 128], xpad[:, j, lt * 128 + tap:lt * 128 + tap + 128], dg[:, j * 5 + tap, :],
                                       tap == 0, tap == 4, [xpad.k, dg.k], [pc_.k])
                            yield
                            act(qkv[:, lt, :], pc_[:, 0:384], AF.Silu, [], [pc_.k, qkv_k[lt]])
                            yield
                            act(qkb[:, 0, :], qkv[:, lt, 0:128], AF.Square, [qkv_k[lt]], [qkb.k, rn.k], accum_out=rn[:, lt, 0:1])
                            act(qkb[:, 1, :], qkv[:, lt, 128:256], AF.Square, [qkv_k[lt]], [qkb.k, rn.k], accum_out=rn[:, lt, 1:2])
                            rsq(rn[:, lt, :], rn[:, lt, :], rn[:, lt, :], 1.0, [rn.k, epsb.k])
                            yield
                            ts("dve", qkb[:, 0, :], qkv[:, lt, 128:256], rn[:, lt, 1:2], None, ALU.mult, None, [qkv_k[lt], rn.k], [qkb.k])
                            ts("dve", qkb[:, 1, :], qkv[:, lt, 0:128], rn[:, lt, 0:1], 128.0 ** -0.5, ALU.mult, ALU.mult,
                               [qkv_k[lt], rn.k], [qkb.k])
                            yield
                            pt = psT[0]
                            tr(pt[:, 0:128], qkb[:, 0, :], identb[:], [qkb.k, identb.k], [pt.k])
                            tr(pt[:, 128:256], qkb[:, 1, :], identb[:], [qkb.k, identb.k], [pt.k])
                            cp("act", kqT[:, lt, :], pt[:, 0:256], [], [pt.k, kq_k[lt]])
                            ready.add(lt)
                            yield
                    for d in range(2):
                        if ctx:
                            P.dma("sp", Sf[d][:], I["sgdn"][e, d, h], writes=[Sf[d].k])
                            cp("act", Sb[d][:], Sf[d][:], [Sf[d].k], [Sb[d].k])
                        else:
                            mset("pool", Sf[d][:], 0.0, [Sf[d].k])
                            mset("pool", Sb[d][:], 0.0, [Sb[d].k])

                    def chain(d, cx):
                        order = range(nt_) if d == 0 else reversed(range(nt_))
                        for lt in order:
                            while lt not in ready:
                                yield
                            yield from gdn_unit(cx, qkv, kqT, rn, oacc, oacc_k, (kq_k, qkv_k), t0 + lt, lt, h, d)

                    interleave([prologue(), chain(0, cxs[0]), chain(1, cxs[1])], [0, 0, 9])
                    if not ctx:
                        for d in range(2):
                            r = P.dma("sp", O["o_gdn"][si, e, d, h], Sf[d][:], reads=[Sf[d].k])
                            out_refs.append(r)
                    finish_seq(oacc, list(oacc_k[0:nt_]), gdn_gb, zt, zt.k, h, t0, nt_, osq, ofbb)
                with P.phase():
                    qkvd = B(P, [128, 512]); qkr = B(P, [128, 256]); qkdb = B(P, [128, 256], BF16)
                    QT = B(P, [128, 1024], BF16); KTb = B(P, [128, 1280], BF16); Vb = B(P, [128, 10, 128], BF16)
                    zd = B(P, [128, 8, 128], BF16); ckb = B(P, [128, 2, 128], BF16)
                    PT = [B(P, [128, 512], BF16) for i in range(2)]
                    rope_t = [B(P, [128, 128]) for i in range(2)]
                    oatt = B(P, [128, 8, 128]); osq = B(P, [128, 8, 128]); ofbb = B(P, [128, 8, 128], BF16)
                    rot = B(P, [128, 256])
                    rot5 = rot[:].rearrange("p (a q w c) -> p a q w c", a=4, q=2, w=2)
                    nk = nt_ + (2 if ctx else 0)
                    for lt in range(nt_):
                        tg = t0 + lt
                        pp = next_psP()
                        for kt in range(KT):
                            mm(pp[:], hT[:, kt, tg * 128:(tg + 1) * 128], wd[:, kt, :], kt == 0, kt == KT - 1, [hT.k, wd.k], [pp.k])
                        cp("act", qkvd[:, 0:384], pp[:, 0:384], [], [pp.k, qkvd.k])
                        act(zd[:, lt, :], pp[:, 384:512], AF.Silu, [], [pp.k, zd.k])
                        cp("pool", Vb[:, lt, :], qkvd[:, 256:384], [qkvd.k], [Vb.k])
                        if not ctx:
                            r = P.dma("sp", O["o_ck"][si, e, lt * 128:(lt + 1) * 128, h, :], qkvd[:, 128:256], reads=[qkvd.k])
                            out_refs.append(r)
                            r = P.dma("sp", O["o_cv"][si, e, lt * 128:(lt + 1) * 128, h, :], qkvd[:, 256:384], reads=[qkvd.k])
                            out_refs.append(r)
                            cp("dve", qkdb[:], qkvd[:, 0:256], [qkvd.k], [qkdb.k])
                        else:
                            rope = rope_t[lt % 2]
                            P.dma("sp", rope[:], I["k_rope"][lt * 128:(lt + 1) * 128].rearrange("p a b -> p (a b)"), writes=[rope.k])
                            xv = qkvd[:, 0:256].rearrange("p (a b) -> p a b", b=64)
                            rv = qkr[:].rearrange("p (a b) -> p a b", b=64)
                            cosb = rope[:, 0:64].unsqueeze(1).to_broadcast([128, 4, 64])
                            tt("dve", rv, xv, cosb, ALU.mult, [qkvd.k, rope.k], [qkr.k])
                            x5 = qkvd[:, 0:256].rearrange("p (a q w c) -> p a q w c", a=4, q=2, w=2)
                            s5 = rope[:, 64:128].rearrange("p (q w c) -> p q w c", q=2, w=2)
                            for w_ in range(2):
                                tt("dve", rot5[:, :, :, w_, :], x5[:, :, :, 1 - w_, :],
                                   s5[:, :, w_, :].unsqueeze(1).to_broadcast([128, 4, 2, 16]), ALU.mult, [qkvd.k, rope.k], [rot.k])
                            tt("dve", qkdb[:], qkr[:], rot[:], ALU.add, [qkr.k, rot.k], [qkdb.k])
                        pt = psT[0]
                        tr(pt[:, 0:128], qkdb[:, 0:128], identb[:], [qkdb.k, identb.k], [pt.k])
                        tr(pt[:, 128:256], qkdb[:, 128:256], identb[:], [qkdb.k, identb.k], [pt.k])
                        cp("act", QT[:, lt * 128:(lt + 1) * 128], pt[:, 0:128], [], [pt.k, QT.k])
                        cp("act", KTb[:, lt * 128:(lt + 1) * 128], pt[:, 128:256], [], [pt.k, KTb.k])
                    if ctx:
                        P.dma("pool", ckb[:], I["ck"][e, :, h, :].rearrange("(n p) d -> p n d", p=128), writes=[ckb.k])
                        P.dma("pool", Vb[:, 8:10, :], I["cv"][e, :, h, :].rearrange("(n p) d -> p n d", p=128), writes=[Vb.k])
                        pt = psT[0]
                        for n in range(2):
                            tr(pt[:, n * 128:(n + 1) * 128], ckb[:, n, :], identb[:], [ckb.k, identb.k], [pt.k])
                        cp("act", KTb[:, 1024:1280], pt[:, 0:256], [], [pt.k, KTb.k])
                    pi = 0
                    for q0 in range(0, T, 512):
                        nq = min(512, T - q0)
                        nqt = nq // 128
                        accA, accB, accS = psC[0], psC[1], psE
                        for kk in range(nk):
                            pts = []
                            for m in range(2):
                                pp = next_psP()
                                mm(pp[:, 0:nq], KTb[m * 64:(m + 1) * 64, kk * 128:(kk + 1) * 128], QT[m * 64:(m + 1) * 64, q0:q0 + nq],
                                   True, True, [KTb.k, QT.k], [pp.k])
                                p_ = PT[pi % 2]
                                pi += 1
                                act(p_[:, 0:nq], pp[:, 0:nq], AF.Exp, [], [pp.k, p_.k], scale=0.125)
                                pts.append(p_)
                            for m in range(2):
                                acc = accA if m == 0 else accB
                                for qt in range(nqt):
                                    mm(acc[:, qt * 128:(qt + 1) * 128], pts[m][:, qt * 128:(qt + 1) * 128], Vb[:, kk, :],
                                       kk == 0 and qt == 0, kk == nk - 1, [pts[m].k, Vb.k], [acc.k])
                                    mm(accS[:, qt * 2 + m:qt * 2 + m + 1], pts[m][:, qt * 128:(qt + 1) * 128], onesb[:, 0:1],
                                       kk == 0 and qt == 0 and m == 0, kk == nk - 1, [pts[m].k, onesb.k], [accS.k])
                        P.op("dve", lambda e_, nqt=nqt, accS=accS: e_.reciprocal(rs[:, 4:4 + 2 * nqt], accS[:, 0:2 * nqt]), reads=[], writes=[accS.k, rs.k])
                        for qt in range(nqt):
                            lt = q0 // 128 + qt
                            ts("dve", rs[:, 13:14], rs[:, 5 + 2 * qt:6 + 2 * qt], lamv[:, 3:4], None, ALU.mult, None, [lamv.k], [rs.k])
                            ts("dve", ofin[:], accA[:, qt * 128:(qt + 1) * 128], rs[:, 4 + 2 * qt:5 + 2 * qt], None, ALU.mult, None,
                               [rs.k], [accA.k, ofin.k])
                            stt("dve", oatt[:, lt, :], accB[:, qt * 128:(qt + 1) * 128], rs[:, 13:14], ofin[:], ALU.mult, ALU.add,
                                [rs.k, ofin.k], [accB.k, oatt.k])
                    finish_seq(oatt, [oatt.k], diff_gb, zd, zd.k, 8 + h, t0, nt_, osq, ofbb)

    def odd_layer(l):
      with P.phase():
        od = l // 2
        W = I["w_in_odd"][od]
        glrT = [B(P, [32, 1536], BF16) for r in range(2)]
        wgt = [B(P, [32, 1024], BF16) for r in range(2)]
        wglr = B(P, [128, KT, 32], BF16)
        load_w(wglr, W[:, 6144:6176], 32)
        for r in range(2):
            mset("pool", wgt[r][:], 0.0, [wgt[r].k])
            P.dma("pool", wgt[r][0:16, :], I["w_gate"][od, r], writes=[wgt[r].k])
            P.dma("pool", wgt[r][16:17, :], I["b_gate"][od, r:r + 1, :], writes=[wgt[r].k])
            mset("pool", glrT[r][:], 1.0, [glrT[r].k])
            for b0 in range(0, 1536, 512):
                pp = next_psP()
                for kt in range(KT):
                    mm(pp[0:16, :], wglr[:, kt, r * 16:(r + 1) * 16], hT[:, kt, b0:b0 + 512], kt == 0, kt == KT - 1,
                       [wglr.k, hT.k], [pp.k])
                cp("act", glrT[r][0:16, b0:b0 + 512], pp[0:16, :], [], [pp.k, glrT[r].k])
        with P.phase():
            qk = B(P, [128, 8, 512], BF16); vv = B(P, [128, 8, 256], BF16); zz = B(P, [128, 8, 256], BF16)
            Sg = [[B(P, [128, 256]) for i in range(2)] for d in range(2)]
            Sgb = [[B(P, [128, 256], BF16) for i in range(2)] for d in range(2)]
            og = B(P, [128, 8, 256])
            og_k = [Tok() for _ in range(8)]

            def mk_gla_ctx(X, E, C, T):
                cx = Ctx()
                cx.X, cx.E, cx.C, cx.T = X, E, C, T
                cx.la = B(P, [128, 256]); cx.Bc = B(P, [128, 256]); cx.Bl = B(P, [128, 256]); cx.ex = B(P, [128, 3, 256])
                cx.qt_ = B(P, [128, 256], BF16); cx.kt_ = B(P, [128, 256], BF16); cx.kd = B(P, [128, 256], BF16)
                cx.qkTT = B(P, [128, 2, 256], BF16)
                cx.ATm = B(P, [128, 128], BF16)
                cx.blc = B(P, [128, 2])
                return cx

            cxs = [mk_gla_ctx(psX, psE, psX, psT[0]), mk_gla_ctx(psC[0], psC[1], psC[0], psT[1])]
            qk_k = [Tok() for _ in range(8)]
            vv_k = [Tok() for _ in range(8)]

            def gla_unit(cx, tg, lt, h, d):
                X, E, C, pt = cx.X, cx.E, cx.C, cx.T
                la, Bc, Bl, ex, qt_, kt_, kd, qkTT, ATm, blc = cx.la, cx.Bc, cx.Bl, cx.ex, cx.qt_, cx.kt_, cx.kd, cx.qkTT, cx.ATm, cx.blc
                mm(X[:, 0:256], glrT[d][:, tg * 128:(tg + 1) * 128], wgt[d][:, h * 256:(h + 1) * 256], True, True,
                   [glrT[d].k, wgt[d].k], [X.k])
                yield
                act(la[:], X[:, 0:256], AF.Exp, [], [X.k, la.k], scale=-1.0)
                act(la[:], la[:], AF.Ln, [], [la.k], bias=1.0)
                yield
                ts("dve", la[:], la[:], -1.0 / 16.0, None, ALU.mult, None, [], [la.k])
                yield
                mm(X[:, 0:256], trif[:, d, :], la[:], True, True, [trif.k, la.k], [X.k])
                mm(X[:, 256:512], onesf[:], la[:], True, True, [onesf.k, la.k], [X.k])
                for dt_ in range(2):
                    mm(E[:, dt_:dt_ + 1], la[:, dt_ * 128:(dt_ + 1) * 128], onesf[:, 0:1], True, True, [la.k, onesf.k], [E.k])
                yield
                cp("dve", Bc[:], X[:, 0:256], [], [X.k, Bc.k])
                tt("dve", Bl[:], X[:, 256:512], Bc[:], ALU.subtract, [Bc.k], [X.k, Bl.k])
                act(blc[:], E[:, 0:2], AF.Exp, [], [E.k, blc.k])
                yield
                act(ex[:, 0, :], Bc[:], AF.Exp, [Bc.k], [ex.k])
                act(ex[:, 1, :], Bc[:], AF.Exp, [Bc.k], [ex.k], scale=-1.0)
                act(ex[:, 2, :], Bl[:], AF.Exp, [Bl.k], [ex.k])
                yield
                stt("dve", qt_[:], qk[:, lt, 0:256], 256.0 ** -0.5, ex[:, 0, :], ALU.mult, ALU.mult, [qk_k[lt], ex.k], [qt_.k])
                tt("pool", kt_[:], qk[:, lt, 256:512], ex[:, 1, :], ALU.mult, [qk_k[lt], ex.k], [kt_.k])
                tt("pool", kd[:], qk[:, lt, 256:512], ex[:, 2, :], ALU.mult, [qk_k[lt], ex.k], [kd.k])
                yield
                for dt_ in range(2):
                    tr(pt[:, dt_ * 256:dt_ * 256 + 128], kt_[:, dt_ * 128:(dt_ + 1) * 128], identb[:], [kt_.k, identb.k], [pt.k])
                    tr(pt[:, dt_ * 256 + 128:dt_ * 256 + 256], qt_[:, dt_ * 128:(dt_ + 1) * 128], identb[:], [qt_.k, identb.k], [pt.k])
                yield
                cp("act", qkTT[:].rearrange("p a b -> p (a b)"), pt[:, 0:512], [], [pt.k, qkTT.k])
                yield
                for dt_ in range(2):
                    mm(E[:, 128:256], qkTT[:, dt_, 0:128], qkTT[:, dt_, 128:256], dt_ == 0, dt_ == 1, [qkTT.k], [E.k])
                yield
                tt("dve", ATm[:], E[:, 128:256], m01[:, d, :], ALU.mult, [m01.k], [E.k, ATm.k])
                yield
                for dt_ in range(2):
                    mm(E[:, 256:512], qkTT[:, dt_, 128:256], Sgb[d][dt_][:], dt_ == 0, False, [qkTT.k, Sgb[d][dt_].k], [E.k])
                mm(E[:, 256:512], ATm[:], vv[:, lt, :], False, True, [ATm.k, vv_k[lt]], [E.k])
                yield
                tt("dve", og[:, lt, :], E[:, 256:512], og[:, lt, :], ALU.add, [], [E.k, og_k[lt]])
                yield
                for dt_ in range(2):
                    mm(C[:, 0:256], kd[:, dt_ * 128:(dt_ + 1) * 128], vv[:, lt, :], True, True, [kd.k, vv_k[lt]], [C.k])
                    yield
                    stt("dve", Sg[d][dt_][:], Sg[d][dt_][:], blc[:, dt_:dt_ + 1], C[:, 0:256], ALU.mult, ALU.add, [blc.k],
                        [C.k, Sg[d][dt_].k])
                    yield
                    cp("act", Sgb[d][dt_][:], Sg[d][dt_][:], [Sg[d][dt_].k], [Sgb[d][dt_].k])
                    yield

            for h in range(4):
                for e2 in range(2):
                    wq = next_wb()
                    load_w(wq, W[:, h * 256:(h + 1) * 256], 256, col0=0)
                    load_w(wq, W[:, 1024 + h * 256:1024 + (h + 1) * 256], 256, col0=256)
                    wv = next_wb()
                    load_w(wv, W[:, 2048 + h * 512 + e2 * 256:2048 + h * 512 + (e2 + 1) * 256], 256, col0=0)
                    load_w(wv, W[:, 4096 + h * 512 + e2 * 256:4096 + h * 512 + (e2 + 1) * 256], 256, col0=256)
                    for si, (t0, nt_, g) in enumerate(SEQS):
                        ctx = (g == 1)
                        ready = set()

                        def prologue():
                            order = []
                            for i_ in range((nt_ + 1) // 2):
                                order.append(i_)
                                if nt_ - 1 - i_ != i_:
                                    order.append(nt_ - 1 - i_)
                            for lt in order:
                                tg = t0 + lt
                                for wi, wsrc in enumerate((wq, wv)):
                                    pp = next_psP()
                                    for kt in range(KT):
                                        mm(pp[:], hT[:, kt, tg * 128:(tg + 1) * 128], wsrc[:, kt, :], kt == 0, kt == KT - 1, [hT.k, wsrc.k], [pp.k])
                                    yield
                                    if wi == 0:
                                        cp("act", qk[:, lt, :], pp[:], [], [pp.k, qk_k[lt]])
                                    else:
                                        cp("act", vv[:, lt, :], pp[:, 0:256], [], [pp.k, vv_k[lt]])
                                        act(zz[:, lt, :], pp[:, 256:512], AF.Silu, [], [pp.k, zz.k])
                                    yield
                                ready.add(lt)
                                yield

                        mset("pool", og[:], 0.0, og_k)
                        for d in range(2):
                            for dt_ in range(2):
                                if ctx:
                                    P.dma("sp", Sg[d][dt_][:], I["sgla"][od, d, h, dt_ * 128:(dt_ + 1) * 128, e2 * 256:(e2 + 1) * 256],
                                          writes=[Sg[d][dt_].k])
                                    cp("act", Sgb[d][dt_][:], Sg[d][dt_][:], [Sg[d][dt_].k], [Sgb[d][dt_].k])
                                else:
                                    mset("pool", Sg[d][dt_][:], 0.0, [Sg[d][dt_].k])
                                    mset("pool", Sgb[d][dt_][:], 0.0, [Sgb[d][dt_].k])

                        def chain(d, cx):
                            order = range(nt_) if d == 0 else reversed(range(nt_))
                            for lt in order:
                                while lt not in ready:
                                    yield
                                yield from gla_unit(cx, t0 + lt, lt, h, d)

                        interleave([prologue(), chain(0, cxs[0]), chain(1, cxs[1])], [0, 0, 7])
                        if not ctx:
                            for d in range(2):
                                for dt_ in range(2):
                                    r = P.dma("sp", O["o_gla"][si, od, d, h, dt_ * 128:(dt_ + 1) * 128, e2 * 256:(e2 + 1) * 256],
                                              Sg[d][dt_][:], reads=[Sg[d][dt_].k])
                                    out_refs.append(r)
                        for lt in range(nt_):
                            P.dma("sp", gbuf[(t0 + lt) * 128:(t0 + lt + 1) * 128, h, e2 * 256:(e2 + 1) * 256], og[:, lt, :],
                                  reads=[og_k[lt]], writes=[gbuf_k[t0 + lt]])
                            P.dma("sp", zbuf[(t0 + lt) * 128:(t0 + lt + 1) * 128, h, e2 * 256:(e2 + 1) * 256], zz[:, lt, :],
                                  reads=[zz.k], writes=[zbuf_k[t0 + lt]])
        with P.phase():
            gfull = B(P, [128, 512]); ofl = B(P, [128, 512]); zfl = B(P, [128, 512], BF16); ofb2 = B(P, [128, 512], BF16)
            P.dma("sp", gfull[:], I["gla_norm"][od].partition_broadcast(128), writes=[gfull.k])
            for tg in range(NTT):
                for h in range(4):
                    P.dma("sp", ofl[:], gbuf[tg * 128:(tg + 1) * 128, h, :], reads=[gbuf_k[tg]], writes=[ofl.k])
                    P.dma("sp", zfl[:], zbuf[tg * 128:(tg + 1) * 128, h, :], reads=[zbuf_k[tg]], writes=[zfl.k])
                    act(ofb2[:], ofl[:], AF.Square, [ofl.k], [ofb2.k, rs.k], accum_out=rs[:, 0:1])
                    rsq(rs[:, 2:3], rs[:, 1:2], rs[:, 0:1], 1.0 / 512, [rs.k, epsb.k])
                    stt("dve", ofl[:], ofl[:], rs[:, 2:3], gfull[:], ALU.mult, ALU.mult, [gfull.k, rs.k], [ofl.k])
                    tt("dve", ofb2[:], ofl[:], zfl[:], ALU.mult, [zfl.k, ofl.k], [ofb2.k])
                    pt = psT[1]
                    for j in range(4):
                        tr(pt[:, j * 128:(j + 1) * 128], ofb2[:, j * 128:(j + 1) * 128], identb[:], [ofb2.k, identb.k], [pt.k])
                    for j in range(4):
                        cp("act", oT[:, h * 4 + j, tg * 128:(tg + 1) * 128], pt[:, j * 128:(j + 1) * 128], [], [pt.k, oT.k])

    gbuf = nc.dram_tensor("gbuf", [1536, 4, 512], F32, kind="Internal").ap()
    zbuf = nc.dram_tensor("zbuf", [1536, 4, 512], BF16, kind="Internal").ap()
    gbuf_k = [Tok() for _ in range(NTT)]
    zbuf_k = [Tok() for _ in range(NTT)]

    for l in range(n_layers):
        adaln(l)
        build_hT(l)
        if l % 2 == 0:
            even_layer(l)
        else:
            odd_layer(l)
        if dbg and l < 2:
            out_refs.append(P.dma("sp", O[f"dbg_hT{l}"], hT[:], reads=[hT.k]))
            out_refs.append(P.dma("sp", O[f"dbg_oT{l}"], oT[:], reads=[oT.k]))
        out_proj(l, I["w_out_even"][l // 2] if l % 2 == 0 else I["w_out_odd"][l // 2])
    P.wait("sp", out_refs)
    P.emit()
    return nc, P


def make_in_maps(inp):
    consts = host_consts()
    f = lambda a: np.ascontiguousarray(np.asarray(a, dtype=np.float32))
    shared = {
        "w_ada": f(inp["w_ada"]),
        "b_adaT": f(inp["b_ada"].reshape(4, 48, 128).transpose(0, 2, 1)),
        "b_ada": f(inp["b_ada"]),
        "npreT": f(inp["norm_pre"].reshape(4, 16, 128).transpose(0, 2, 1)),
        "npost": f(inp["norm_post"]),
        "w_in_even": f(inp["w_in_even"]),
        "convT": f(inp["conv_even"].reshape(2, 5, 24, 128).transpose(0, 3, 2, 1)),
        "a_log": f(inp["a_log_even"].reshape(2, 16)),
        "dt_bias": f(inp["dt_bias_even"].reshape(2, 16)),
        "gdn_norm": f(inp["gdn_norm_even"]),
        "lam": f(inp["lam_even"].reshape(2, 256)),
        "diff_norm": f(inp["diff_norm_even"]),
        "w_out_even": f(inp["w_out_even"]),
        "w_in_odd": f(inp["w_in_odd"]),
        "w_gate": f(inp["w_gate_odd"]),
        "b_gate": f(inp["b_gate_odd"]),
        "gla_norm": f(inp["gla_norm_odd"]),
        "w_out_odd": f(inp["w_out_odd"]),
    }
    shared.update(consts)
    maps = []
    for c in range(8):
        bs = c % 2
        m = dict(shared)
        m["xp"] = f(inp["x_prompt"][2 * c:2 * c + 2].reshape(512, D))
        m["xs"] = f(inp["x_sample"][bs])
        cv2 = np.stack([np.asarray(inp["c_ctx"]), np.asarray(inp["c"][bs])], axis=0)
        m["cT"] = f(cv2.reshape(2, 16, 128).transpose(2, 1, 0))
        m["sgdn"] = f(inp["state_gdn"][bs])
        m["ck"] = f(inp["cache_k"][bs])
        m["cv"] = f(inp["cache_v"][bs])
        m["sgla"] = f(inp["state_gla"][bs])
        maps.append(m)
    return maps


_CACHE = {}


def kernel(**inputs):
    if "nc" not in _CACHE:
        _CACHE["nc"] = build()[0]
    nc = _CACHE["nc"]
    maps = make_in_maps(inputs)
    res = run_bass_kernel_spmd(nc, maps, core_ids=list(range(8)))
    R = res.results
    y_p = np.concatenate([R[c]["yp"].reshape(2, 256, D) for c in range(8)], axis=0)
    y_s = np.stack([R[0]["ys"], R[1]["ys"]], axis=0)
    gdn = np.concatenate([R[c]["o_gdn"] for c in range(8)], axis=0)
    ck = np.concatenate([R[c]["o_ck"] for c in range(8)], axis=0)
    cv = np.concatenate([R[c]["o_cv"] for c in range(8)], axis=0)
    gla = np.concatenate([R[c]["o_gla"] for c in range(8)], axis=0)
    return (y_p.astype(np.float32), y_s.astype(np.float32), gdn.astype(np.float32), ck.astype(np.float32),
            cv.astype(np.float32), gla.astype(np.float32))
```

```python
import math
import numpy as np
import concourse.bass as bass
import concourse.mybir as mybir
from concourse.bass_utils import run_bass_kernel_spmd
from contextlib import ExitStack

F32 = mybir.dt.float32
BF16 = mybir.dt.bfloat16
AF = mybir.ActivationFunctionType
ALU = mybir.AluOpType
AX = mybir.AxisListType

ENGS = ("pe", "act", "dve", "pool", "sp")
N_DMA_SLOTS = 12
EPS = 1e-6
NEG = -30000.0


class Tok:
    __slots__ = ("w", "r", "name")

    def __init__(self, name=""):
        self.w = None
        self.r = {}
        self.name = name


class Prog:
    def __init__(self, nc):
        self.nc = nc
        self.es = ExitStack()
        self.q = {e: [] for e in ENGS}
        self.dmas = []
        self.n_sb = 0
        self.stack = [self.es]
        self.phase_dmas = []

    def sb(self, shape, dt=F32, name=None):
        self.n_sb += 1
        return self.stack[-1].enter_context(self.nc.sbuf_tensor(f"sb{self.n_sb}" + (name or ""), list(shape), dt))

    def ps(self, shape, dt=F32, name=None):
        self.n_sb += 1
        return self.es.enter_context(self.nc.psum_tensor(name or f"ps{self.n_sb}", list(shape), dt))

    def _deps(self, reads, writes):
        deps = set()
        for t in reads:
            if t.w is not None:
                deps.add(t.w)
        for t in writes:
            if t.w is not None:
                deps.add(t.w)
            deps.update(t.r.values())
        return deps

    def _commit(self, ref, key, reads, writes):
        for t in reads:
            t.r[key] = ref
        for t in writes:
            t.w = ref
            t.r = {}

    def op(self, eng, fn, reads=(), writes=()):
        deps = self._deps(reads, writes)
        ref = (eng, len(self.q[eng]))
        self.q[eng].append(dict(fn=fn, deps=deps, dma=None, marked=False))
        self._commit(ref, eng, reads, writes)
        return ref

    def dma(self, qeng, out, in_, reads=(), writes=(), **kw):
        deps = self._deps(reads, writes)
        did = len(self.dmas)
        ref = ("dma", did)
        self.dmas.append(dict(q=qeng))
        fn = lambda e, out=out, in_=in_, kw=kw: e.dma_start(out=out, in_=in_, **kw)
        self.q[qeng].append(dict(fn=fn, deps=deps, dma=did, marked=False))
        self._commit(ref, ("dma", did), reads, writes)
        self.phase_dmas.append(ref)
        return ref

    def barrier(self):
        refs = [(e, len(self.q[e]) - 1) for e in ENGS if len(self.q[e]) > 0 and self.q[e][-1]["fn"] is not None]
        refs = [r for r in refs if self.q[r[0]][r[1]]["dma"] is None] + list(self.phase_dmas)
        cr = []
        for e in ENGS:
            for i in range(len(self.q[e]) - 1, -1, -1):
                o = self.q[e][i]
                if o["fn"] is not None and o["dma"] is None:
                    cr.append((e, i))
                    break
        refs = cr + list(self.phase_dmas)
        for e in ENGS:
            self.wait(e, refs)
        self.phase_dmas = []

    def phase(self):
        prog = self

        class _Ph:
            def __enter__(s2):
                st = ExitStack()
                prog.stack.append(st)
                return st

            def __exit__(s2, *a):
                prog.barrier()
                prog.stack.pop().close()
                return False

        return _Ph()

    def wait(self, eng, refs):
        self.q[eng].append(dict(fn=None, deps=set(refs), dma=None, marked=False))

    def emit(self):
        nc = self.nc
        q = self.q
        for e in ENGS:
            for o in q[e]:
                for d in o["deps"]:
                    if d[0] != "dma":
                        q[d[0]][d[1]]["marked"] = True
        for e in ENGS:
            c = 0
            for o in q[e]:
                if o["marked"]:
                    c += 1
                o["cnt"] = c
        sems = {e: self.es.enter_context(nc.semaphore("s_" + e)) for e in ENGS}
        dsem, slot_val, slot_rr = {}, {}, {}
        for e in ENGS:
            if any(o["dma"] is not None for o in q[e]):
                dsem[e] = [self.es.enter_context(nc.semaphore(f"d_{e}{i}")) for i in range(N_DMA_SLOTS)]
                slot_val[e] = [0] * N_DMA_SLOTS
                slot_rr[e] = 0
        for e in ENGS:
            for o in q[e]:
                if o["dma"] is not None:
                    s = slot_rr[e]
                    slot_rr[e] = (s + 1) % N_DMA_SLOTS
                    d = self.dmas[o["dma"]]
                    d["slot"] = s
                    d["prev"] = slot_val[e][s]
                    slot_val[e][s] += 16
                    d["val"] = slot_val[e][s]
        stats = {e: [0, 0] for e in ENGS}

        def run(e, eng):
            waited = {}

            def need(key, sem, val):
                if waited.get(key, 0) >= val:
                    return
                waited[key] = val
                eng.wait_ge(sem, val)
                stats[e][1] += 1

            for o in q[e]:
                cw = {}
                for d in o["deps"]:
                    if d[0] == "dma":
                        dd = self.dmas[d[1]]
                        need(("d", dd["q"], dd["slot"]), dsem[dd["q"]][dd["slot"]], dd["val"])
                    else:
                        if d[0] == e and e == "pe":
                            continue
                        v = q[d[0]][d[1]]["cnt"]
                        if cw.get(d[0], 0) < v:
                            cw[d[0]] = v
                for e2, v in cw.items():
                    need(("c", e2), sems[e2], v)
                if o["dma"] is not None:
                    dd = self.dmas[o["dma"]]
                    if dd["prev"] > 0:
                        need(("d", e, dd["slot"]), dsem[e][dd["slot"]], dd["prev"])
                if o["fn"] is None:
                    continue
                ins = o["fn"](eng)
                stats[e][0] += 1
                if o["dma"] is not None:
                    dd = self.dmas[o["dma"]]
                    ins.then_inc(dsem[e][dd["slot"]], 16)
                elif o["marked"]:
                    ins.then_inc(sems[e], 1)

        with nc.Block() as block:
            @block.sync
            def _(eng):
                run("sp", eng)

            @block.scalar
            def _(eng):
                run("act", eng)

            @block.vector
            def _(eng):
                run("dve", eng)

            @block.gpsimd
            def _(eng):
                run("pool", eng)

            @block.tensor
            def _(eng):
                run("pe", eng)
        self.stats = stats
        self.es.close()


class B:
    def __init__(self, P, shape, dt=F32, ps=False, name=None):
        self.t = P.ps(shape, dt, name) if ps else P.sb(shape, dt, name)
        self.k = Tok(name or "")

    def __getitem__(self, idx):
        return self.t[idx]


D = 2048
KT = 16
NTT = 12
SEQS = [(0, 2, 0), (2, 2, 0), (4, 8, 1)]
DEPTH = 4


def host_consts():
    c = {}
    c["k_ident"] = np.eye(128, dtype=np.float32)
    k = np.arange(128)
    tri = (k[:, None] <= k[None, :]).astype(np.float32)
    c["k_tri"] = np.stack([tri, tri.T.copy()], axis=1).astype(np.float32)
    c["k_ones"] = np.ones((128, 128), np.float32)
    j = k[:, None]
    i = k[None, :]
    negm = np.zeros((128, 2, 2, 128), np.float32)
    negm[:, 0, 0] = np.where(i >= j, 0.0, NEG)
    negm[:, 0, 1] = np.where(i > j, 0.0, NEG)
    negm[:, 1, 0] = np.where(i <= j, 0.0, NEG)
    negm[:, 1, 1] = np.where(i < j, 0.0, NEG)
    c["k_negm"] = negm.reshape(128, 2, 256)
    sel = np.zeros((16, 16, 128), np.float32)
    for r in range(16):
        sel[r, r, :] = 1.0
    c["k_sel"] = sel
    m01 = np.zeros((128, 2, 128), np.float32)
    m01[:, 0] = (i >= j)
    m01[:, 1] = (i <= j)
    c["k_m01"] = m01
    bm = np.zeros((128, 5, 128), np.float32)
    bm[:, 0] = (j // 8 == i // 8)
    for li, s_ in enumerate((8, 16, 32, 64)):
        bm[:, 1 + li] = (j // (2 * s_) == i // (2 * s_)) & (j // s_ != i // s_)
    c["k_bm"] = bm
    t = np.arange(1024)
    row = (t // 64).astype(np.float32)
    col = (t % 64).astype(np.float32)
    freqs = (10000.0 ** (-np.arange(16, dtype=np.float32) / 16)).astype(np.float32)
    ar = row[:, None] * freqs
    ac = col[:, None] * freqs
    ang = np.concatenate([ar, ar, ac, ac], axis=-1).astype(np.float32)
    cos = np.cos(ang).astype(np.float32)
    sin = np.sin(ang).astype(np.float32)
    ssin = sin.copy()
    ssin[:, 0:16] *= -1.0
    ssin[:, 32:48] *= -1.0
    c["k_rope"] = np.stack([cos, ssin], axis=1).astype(np.float32)
    return c


IN_SPECS = [
    ("xp", [512, D]), ("xs", [1024, D]), ("cT", [128, 16, 2]),
    ("sgdn", [2, 2, 8, 128, 128]), ("ck", [2, 256, 8, 128]), ("cv", [2, 256, 8, 128]),
    ("sgla", [2, 2, 4, 256, 512]),
    ("w_ada", [4, D, 6144]), ("b_adaT", [4, 128, 48]), ("b_ada", [4, 6144]),
    ("npreT", [4, 128, 16]), ("npost", [4, D]),
    ("w_in_even", [2, D, 8224]), ("convT", [2, 128, 24, 5]), ("a_log", [2, 16]), ("dt_bias", [2, 16]),
    ("gdn_norm", [2, 128]), ("lam", [2, 256]), ("diff_norm", [2, 128]), ("w_out_even", [2, D, D]),
    ("w_in_odd", [2, D, 6176]), ("w_gate", [2, 2, 16, 1024]), ("b_gate", [2, 2, 1024]),
    ("gla_norm", [2, 512]), ("w_out_odd", [2, D, D]),
    ("k_ident", [128, 128]), ("k_tri", [128, 2, 128]), ("k_ones", [128, 128]), ("k_negm", [128, 2, 256]),
    ("k_sel", [16, 16, 128]), ("k_m01", [128, 2, 128]), ("k_rope", [1024, 2, 64]), ("k_bm", [128, 5, 128]),
]
OUT_SPECS = [
    ("yp", [512, D]), ("ys", [1024, D]), ("o_gdn", [2, 2, 2, 8, 128, 128]),
    ("o_ck", [2, 2, 256, 8, 128]), ("o_cv", [2, 2, 256, 8, 128]), ("o_gla", [2, 2, 2, 4, 256, 512]),
]


def build(n_layers=DEPTH, dbg=False):
    nc = bass.Bass("TRN2", target_bir_lowering=False)
    I = {n: nc.dram_tensor(n, s, F32, kind="ExternalInput").ap() for n, s in IN_SPECS}
    O = {n: nc.dram_tensor(n, s, F32, kind="ExternalOutput").ap() for n, s in OUT_SPECS}
    ybuf = nc.dram_tensor("ybuf", [2, 1536, D], F32, kind="Internal").ap()
    obuf = nc.dram_tensor("obuf", [1536, D], F32, kind="Internal").ap()
    if dbg:
        O["dbg_y0"] = nc.dram_tensor("dbg_y0", [1536, D], F32, kind="ExternalOutput").ap()
        for l_ in range(min(2, n_layers)):
            O[f"dbg_hT{l_}"] = nc.dram_tensor(f"dbg_hT{l_}", [128, KT, 1536], BF16, kind="ExternalOutput").ap()
            O[f"dbg_oT{l_}"] = nc.dram_tensor(f"dbg_oT{l_}", [128, KT, 1536], BF16, kind="ExternalOutput").ap()
    P = Prog(nc)
    out_refs = []

    def mm(out, lhsT, rhs, start, stop, R, W):
        P.op("pe", lambda e: e.matmul(out, lhsT, rhs, start=start, stop=stop), reads=R, writes=W)

    def tr(out, in_, ident, R, W):
        P.op("pe", lambda e: e.transpose(out, in_, ident), reads=R, writes=W)

    def act(out, in_, func, R, W, **kw):
        P.op("act", lambda e: e.activation(out=out, in_=in_, func=func, **kw), reads=R, writes=W)

    def ts(eng, out, in0, s1, s2, op0, op1, R, W):
        if s2 is None:
            P.op(eng, lambda e: e.tensor_scalar(out, in0, s1, None, op0), reads=R, writes=W)
        else:
            P.op(eng, lambda e: e.tensor_scalar(out, in0, s1, s2, op0, op1), reads=R, writes=W)

    def tt(eng, out, in0, in1, op, R, W):
        P.op(eng, lambda e: e.tensor_tensor(out, in0, in1, op), reads=R, writes=W)

    def stt(eng, out, in0, sc_, in1, op0, op1, R, W):
        P.op(eng, lambda e: e.scalar_tensor_tensor(out, in0, sc_, in1, op0, op1), reads=R, writes=W)

    def cp(eng, out, in_, R, W):
        if eng == "act":
            P.op("act", lambda e: e.copy(out, in_), reads=R, writes=W)
        else:
            P.op(eng, lambda e: e.tensor_copy(out, in_), reads=R, writes=W)

    def red(eng, out, in_, R, W):
        P.op(eng, lambda e: e.reduce_sum(out, in_, AX.X), reads=R, writes=W)

    def rsq(out, tmp, in_, scale, W):
        act(tmp, in_, AF.Ln, [], W, scale=scale, bias=epsb[:, 0:1])
        act(out, tmp, AF.Exp, [], W, scale=-0.5)

    def mset(eng, ap, v, W):
        P.op(eng, lambda e: e.memset(ap, v), writes=W)

    identf = B(P, [128, 128]); identb = B(P, [128, 128], BF16)
    trif = B(P, [128, 2, 128]); onesf = B(P, [128, 128]); onesb = B(P, [128, 128], BF16)
    negmb = B(P, [128, 2, 256], BF16); self_ = B(P, [16, 16, 128]); m01 = B(P, [128, 2, 128])
    P.dma("sp", identf[:], I["k_ident"], writes=[identf.k])
    P.dma("pool", identb[:], I["k_ident"], writes=[identb.k])
    P.dma("sp", trif[:], I["k_tri"], writes=[trif.k])
    P.dma("sp", onesf[:], I["k_ones"], writes=[onesf.k])
    P.dma("pool", onesb[:], I["k_ones"], writes=[onesb.k])
    P.dma("pool", negmb[:], I["k_negm"], writes=[negmb.k])
    P.dma("sp", self_[:], I["k_sel"], writes=[self_.k])
    P.dma("sp", m01[:], I["k_m01"], writes=[m01.k])

    hT = B(P, [128, KT, 1536], BF16, name="hT")
    oT = B(P, [128, KT, 1536], BF16, name="oT")
    WB = [B(P, [128, KT, 512], BF16, name=f"wb{i}") for i in range(2)]
    wb_i = [0]
    psP = [B(P, [128, 512], ps=True, name=f"psP{i}") for i in range(2)]
    psT = [B(P, [128, 1024], BF16, ps=True, name=f"psT{i}") for i in range(2)]
    psX = B(P, [128, 512], ps=True, name="psX")
    psE = B(P, [128, 512], ps=True, name="psE")
    psC = [B(P, [128, 512], ps=True, name=f"psC{i}") for i in range(2)]
    psP_i = [0]

    def next_wb():
        b = WB[wb_i[0] % 2]
        wb_i[0] += 1
        return b

    def load_w(dst, src, ncols, col0=0):
        v = src.rearrange("(kt p) c -> p kt c", p=128)
        for q4 in range(4):
            P.dma("pool", dst.t[:, q4 * 4:(q4 + 1) * 4, col0:col0 + ncols], v[:, q4 * 4:(q4 + 1) * 4, :], writes=[dst.k])

    def next_psP():
        b = psP[psP_i[0] % 2]
        psP_i[0] += 1
        return b

    sc = B(P, [128, 16, 2], BF16); cTf = B(P, [128, 16, 2])
    P.dma("sp", cTf[:], I["cT"], writes=[cTf.k])
    act(sc[:], cTf[:], AF.Silu, [cTf.k], [sc.k])
    modT = B(P, [128, 32, 2]); bT = B(P, [128, 48]); npreT = B(P, [128, 16])
    gsc = B(P, [128, 16, 2])
    sm = B(P, [128, 16]); rs = B(P, [128, 16])
    epsb = B(P, [128, 1])
    mset("pool", epsb[:], EPS, [epsb.k])
    obuf_k = [Tok() for _ in range(NTT)]
    ybuf_k = [[Tok() for _ in range(NTT)] for _ in range(2)]

    def adaln(l):
        P.dma("sp", bT[:], I["b_adaT"][l], writes=[bT.k])
        P.dma("sp", npreT[:], I["npreT"][l], writes=[npreT.k])
        for blk in range(8):
            wb = next_wb()
            load_w(wb, I["w_ada"][l][:, blk * 512:(blk + 1) * 512], 512)
            for j in range(4):
                for kt in range(KT):
                    mm(psX[:, j * 2:j * 2 + 2], wb[:, kt, j * 128:(j + 1) * 128], sc[:, kt, :], kt == 0, kt == KT - 1,
                       [wb.k, sc.k], [psX.k])
            tt("dve", modT[:, blk * 4:(blk + 1) * 4, :], psX[:, 0:8].rearrange("p (a b) -> p a b", b=2),
               bT[:, blk * 4:(blk + 1) * 4].unsqueeze(2).to_broadcast([128, 4, 2]), ALU.add, [bT.k], [psX.k, modT.k])
        ts("dve", gsc[:], modT[:, 16:32, :], 1.0, None, ALU.add, None, [modT.k], [gsc.k])
        tt("dve", gsc[:], gsc[:], npreT[:].unsqueeze(2).to_broadcast([128, 16, 2]), ALU.mult, [npreT.k], [gsc.k])

    def ysrc(l, tt_):
        if l == 0:
            return I["xp"][tt_ * 128:(tt_ + 1) * 128, :] if tt_ < 4 else I["xs"][(tt_ - 4) * 128:(tt_ - 3) * 128, :]
        return ybuf[(l - 1) % 2, tt_ * 128:(tt_ + 1) * 128, :]

    def ydst(l, tt_):
        if l == n_layers - 1:
            return O["yp"][tt_ * 128:(tt_ + 1) * 128, :] if tt_ < 4 else O["ys"][(tt_ - 4) * 128:(tt_ - 3) * 128, :]
        return ybuf[l % 2, tt_ * 128:(tt_ + 1) * 128, :]

    def build_hT(l):
        with P.phase():
            yt = [B(P, [128, D]) for i in range(2)]
            xh = B(P, [128, D], BF16)
            htmp = [B(P, [128, 8, 128]) for i in range(2)]
            for tt_ in range(NTT):
                g = 0 if tt_ < 4 else 1
                y = yt[tt_ % 2]
                P.dma("sp", y[:], ysrc(l, tt_), reads=[ybuf_k[(l - 1) % 2][tt_]] if l > 0 else [], writes=[y.k])
                act(xh[:], y[:], AF.Square, [y.k], [xh.k, sm.k], accum_out=sm[:, 0:1])
                rsq(sm[:, 2:3], sm[:, 1:2], sm[:, 0:1], 1.0 / D, [sm.k, epsb.k])
                ts("dve", xh[:], y[:], sm[:, 2:3], None, ALU.mult, None, [y.k, sm.k], [xh.k])
                for half in range(2):
                    pt = psT[half]
                    for j in range(8):
                        kt = half * 8 + j
                        tr(pt[:, j * 128:(j + 1) * 128], xh[:, kt * 128:(kt + 1) * 128], identb[:], [xh.k, identb.k], [pt.k])
                    hs = slice(half * 8, (half + 1) * 8)
                    tf = htmp[half]
                    tt("dve", tf[:], pt[:, 0:1024].rearrange("p (a b) -> p a b", b=128),
                       gsc[:, hs, g].unsqueeze(2).to_broadcast([128, 8, 128]), ALU.mult, [gsc.k], [pt.k, tf.k])
                    tt("pool", hT[:, hs, tt_ * 128:(tt_ + 1) * 128], tf[:],
                       modT[:, hs, g].unsqueeze(2).to_broadcast([128, 8, 128]), ALU.add, [tf.k, modT.k], [hT.k])

    def out_proj(l, w_out):
        with P.phase():
            ob = [B(P, [128, 512]) for i in range(2)]
            scb = B(P, [128, 32, 128], BF16)
            gpost = [B(P, [128, D]) for g in range(2)]
            bgb = B(P, [128, 512]); npb = B(P, [128, 512])
            yt = [B(P, [128, D]) for i in range(2)]
            xh = B(P, [128, D], BF16)
            cp("dve", scb[:], sc[:].rearrange("p a b -> p (a b)").unsqueeze(2).to_broadcast([128, 32, 128]), [sc.k], [scb.k])
            for blk in range(4):
                wb = next_wb()
                load_w(wb, I["w_ada"][l][:, 4096 + blk * 512:4096 + (blk + 1) * 512], 512)
                cs = slice(blk * 512, (blk + 1) * 512)
                P.dma("sp", bgb[:], I["b_ada"][l, 4096 + blk * 512:4096 + (blk + 1) * 512].partition_broadcast(128), writes=[bgb.k])
                P.dma("sp", npb[:], I["npost"][l, cs].partition_broadcast(128), writes=[npb.k])
                for g in range(2):
                    pp = next_psP()
                    for kt in range(KT):
                        mm(pp[:], scb[:, kt * 2 + g, :], wb[:, kt, :], kt == 0, kt == KT - 1, [scb.k, wb.k], [pp.k])
                    tt("dve", gpost[g][:, cs], pp[:], bgb[:], ALU.add, [bgb.k], [pp.k, gpost[g].k])
                    tt("pool", gpost[g][:, cs], gpost[g][:, cs], npb[:], ALU.mult, [npb.k], [gpost[g].k])
            oi = 0
            for blk in range(4):
                wb = next_wb()
                load_w(wb, w_out[:, blk * 512:(blk + 1) * 512], 512)
                for tt_ in range(NTT):
                    pp = next_psP()
                    for kt in range(KT):
                        mm(pp[:], oT[:, kt, tt_ * 128:(tt_ + 1) * 128], wb[:, kt, :], kt == 0, kt == KT - 1, [oT.k, wb.k], [pp.k])
                    o = ob[oi % 2]
                    oi += 1
                    cp("act", o[:], pp[:], [], [pp.k, o.k])
                    P.dma("sp", obuf[tt_ * 128:(tt_ + 1) * 128, blk * 512:(blk + 1) * 512], o[:], reads=[o.k], writes=[obuf_k[tt_]])
            for tt_ in range(NTT):
                g = 0 if tt_ < 4 else 1
                o = yt[0]
                y = yt[1]
                P.dma("sp", o[:], obuf[tt_ * 128:(tt_ + 1) * 128, :], reads=[obuf_k[tt_]], writes=[o.k])
                P.dma("sp", y[:], ysrc(l, tt_), reads=[ybuf_k[(l - 1) % 2][tt_]] if l > 0 else [], writes=[y.k])
                act(xh[:], o[:], AF.Square, [o.k], [xh.k, sm.k], accum_out=sm[:, 0:1])
                rsq(sm[:, 2:3], sm[:, 1:2], sm[:, 0:1], 1.0 / D, [sm.k, epsb.k])
                stt("dve", o[:], o[:], sm[:, 2:3], gpost[g][:], ALU.mult, ALU.mult, [sm.k, gpost[g].k], [o.k])
                tt("pool", y[:], y[:], o[:], ALU.add, [o.k], [y.k])
                last = (l == n_layers - 1)
                r = P.dma("sp", ydst(l, tt_), y[:], reads=[y.k], writes=[] if last else [ybuf_k[l % 2][tt_]])
                if last:
                    out_refs.append(r)
                if dbg and l == 0:
                    out_refs.append(P.dma("sp", O["dbg_y0"][tt_ * 128:(tt_ + 1) * 128, :], y[:], reads=[y.k]))

    def interleave(gens, offsets=None):
        gens = list(gens)
        offs = {id(g_): (offsets[i] if offsets else 0) for i, g_ in enumerate(gens)}
        rnd = 0
        while gens:
            for g_ in list(gens):
                if offs[id(g_)] > rnd:
                    continue
                try:
                    next(g_)
                except StopIteration:
                    gens.remove(g_)
            rnd += 1

    class Ctx:
        pass

    def even_layer(l):
      with P.phase():
        e = l // 2
        W = I["w_in_even"][e]
        lam_init = 0.8 - 0.6 * math.exp(-0.3 * l)
        beta = B(P, [128, 1, 16]); nbeta = B(P, [128, NTT, 16]); gg = B(P, [128, 1, 16])
        Gc = B(P, [128, 1, 16]); nG = B(P, [128, NTT, 16])
        eG = B(P, [128, NTT, 16]); kds = B(P, [128, NTT, 16]); gl = B(P, [128, NTT, 16])
        GT = B(P, [16, NTT, 128])
        wba = B(P, [128, KT, 32], BF16)
        alog_b = B(P, [128, 16]); dtb_b = B(P, [128, 16]); nea = B(P, [128, 16])
        convw = B(P, [128, 24, 5])
        gdn_gb = B(P, [128, 128]); diff_gb = B(P, [128, 128]); lam_b = B(P, [128, 256]); lamv = B(P, [128, 4])
        dg = B(P, [128, 15, 128], BF16)
        bmk = B(P, [128, 5, 128], BF16)
        P.dma("pool", bmk[:], I["k_bm"], writes=[bmk.k])
        Sf = [B(P, [128, 128]) for i in range(2)]
        Sb = [B(P, [128, 128], BF16) for i in range(2)]
        ofin = B(P, [128, 128]); ofb = B(P, [128, 128], BF16)

        load_w(wba, W[:, 4096:4128], 32)
        P.dma("sp", alog_b[:], I["a_log"][e].partition_broadcast(128), writes=[alog_b.k])
        P.dma("sp", dtb_b[:], I["dt_bias"][e].partition_broadcast(128), writes=[dtb_b.k])
        P.dma("sp", convw[:], I["convT"][e], writes=[convw.k])
        P.dma("sp", gdn_gb[:], I["gdn_norm"][e].partition_broadcast(128), writes=[gdn_gb.k])
        P.dma("sp", diff_gb[:], I["diff_norm"][e].partition_broadcast(128), writes=[diff_gb.k])
        P.dma("sp", lam_b[:], I["lam"][e].partition_broadcast(128), writes=[lam_b.k])
        act(nea[:], alog_b[:], AF.Exp, [alog_b.k], [nea.k])
        ts("dve", nea[:], nea[:], -1.0, None, ALU.mult, None, [], [nea.k])
        ts("dve", diff_gb[:], diff_gb[:], 1.0 - lam_init, None, ALU.mult, None, [], [diff_gb.k])
        tt("dve", lam_b[:, 0:64], lam_b[:, 0:64], lam_b[:, 64:128], ALU.mult, [], [lam_b.k])
        tt("dve", lam_b[:, 128:192], lam_b[:, 128:192], lam_b[:, 192:256], ALU.mult, [], [lam_b.k])
        red("dve", lamv[:, 0:1], lam_b[:, 0:64], [lam_b.k], [lamv.k])
        red("dve", lamv[:, 1:2], lam_b[:, 128:192], [lam_b.k], [lamv.k])
        act(lamv[:, 0:2], lamv[:, 0:2], AF.Exp, [], [lamv.k])
        tt("dve", lamv[:, 2:3], lamv[:, 1:2], lamv[:, 0:1], ALU.subtract, [], [lamv.k])
        ts("dve", lamv[:, 3:4], lamv[:, 2:3], -lam_init, None, ALU.add, None, [], [lamv.k])
        for tt_ in range(NTT):
            for kt in range(KT):
                mm(psX[:, 0:32], hT[:, kt, tt_ * 128:(tt_ + 1) * 128], wba[:, kt, :], kt == 0, kt == KT - 1, [hT.k, wba.k], [psX.k])
            act(beta[:, 0, :], psX[:, 0:16], AF.Sigmoid, [], [psX.k, beta.k])
            tt("dve", gg[:, 0, :], psX[:, 16:32], dtb_b[:], ALU.add, [dtb_b.k], [psX.k, gg.k])
            act(gg[:, 0, :], gg[:, 0, :], AF.Exp, [], [gg.k])
            act(gg[:, 0, :], gg[:, 0, :], AF.Ln, [], [gg.k], bias=1.0)
            tt("dve", gg[:, 0, :], gg[:, 0, :], nea[:], ALU.mult, [nea.k], [gg.k])
            ts("dve", nbeta[:, tt_, :], beta[:, 0, :], -1.0, None, ALU.mult, None, [beta.k], [nbeta.k])
            mm(psX[:, 32:40], trif[:, 0, :], gg[:, 0, 0:8], True, True, [trif.k, gg.k], [psX.k])
            mm(psX[:, 40:48], trif[:, 1, :], gg[:, 0, 8:16], True, True, [trif.k, gg.k], [psX.k])
            mm(psX[:, 48:64], onesf[:], gg[:, 0, :], True, True, [onesf.k, gg.k], [psX.k])
            cp("dve", Gc[:, 0, :], psX[:, 32:48], [], [psX.k, Gc.k])
            tt("dve", kds[:, tt_, :], psX[:, 48:64], Gc[:, 0, :], ALU.subtract, [Gc.k], [psX.k, kds.k])
            act(gl[:, tt_, :], psX[:, 48:64], AF.Exp, [], [psX.k, gl.k])
            act(kds[:, tt_, :], kds[:, tt_, :], AF.Exp, [], [kds.k])
            act(eG[:, tt_, :], Gc[:, 0, :], AF.Exp, [Gc.k], [eG.k])
            ts("dve", nG[:, tt_, :], Gc[:, 0, :], -1.0, None, ALU.mult, None, [Gc.k], [nG.k])
            tr(psX[0:16, 64:192], Gc[:, 0, :], identf[:], [Gc.k, identf.k], [psX.k])
            cp("dve", GT[:, tt_, :], psX[0:16, 64:192], [], [psX.k, GT.k])

        def finish_o(src, gain, zsrc, feat_kt, tok0, R):
            act(ofb[:], src, AF.Square, R, [ofb.k, rs.k], accum_out=rs[:, 0:1])
            rsq(rs[:, 2:3], rs[:, 1:2], rs[:, 0:1], 1.0 / 128, [rs.k, epsb.k])
            stt("dve", ofin[:], src, rs[:, 2:3], gain[:], ALU.mult, ALU.mult, R + [gain.k, rs.k], [ofin.k])
            tt("dve", ofb[:], ofin[:], zsrc, ALU.mult, R, [ofb.k, ofin.k])
            pt = psT[1]
            tr(pt[:, 0:128], ofb[:], identb[:], [ofb.k, identb.k], [pt.k])
            cp("act", oT[:, feat_kt, tok0:tok0 + 128], pt[:, 0:128], [], [pt.k, oT.k])

        def finish_seq(osrc, R, gain, zsrc, zk, feat_kt, t0_, n_, osq, ofbb):
            tt("dve", osq[:, 0:n_, :], osrc[:, 0:n_, :], osrc[:, 0:n_, :], ALU.mult, R, [osq.k])
            red("dve", rs[:, 4:4 + n_], osq[:, 0:n_, :], [osq.k], [rs.k])
            act(rs[:, 4:4 + n_], rs[:, 4:4 + n_], AF.Ln, [], [rs.k, epsb.k], scale=1.0 / 128, bias=epsb[:, 0:1])
            act(rs[:, 4:4 + n_], rs[:, 4:4 + n_], AF.Exp, [], [rs.k], scale=-0.5)
            tt("dve", osq[:, 0:n_, :], osrc[:, 0:n_, :], rs[:, 4:4 + n_].unsqueeze(2).to_broadcast([128, n_, 128]), ALU.mult,
               R + [rs.k], [osq.k])
            tt("dve", osq[:, 0:n_, :], osq[:, 0:n_, :], gain[:].unsqueeze(1).to_broadcast([128, n_, 128]), ALU.mult, [gain.k], [osq.k])
            tt("dve", ofbb[:, 0:n_, :], osq[:, 0:n_, :], zsrc[:, 0:n_, :], ALU.mult, [osq.k, zk], [ofbb.k])
            pt = psT[1]
            for lt_ in range(n_):
                tr(pt[:, lt_ * 128:(lt_ + 1) * 128], ofbb[:, lt_, :], identb[:], [ofbb.k, identb.k], [pt.k])
            cp("act", oT[:, feat_kt, t0_ * 128:(t0_ + n_) * 128], pt[:, 0:n_ * 128], [], [pt.k, oT.k])

        def gdn_unit(cx, qkv, kqT, rn, oacc, oacc_k, tk, tile_g, lt, h, d):
            c = d * 8 + h
            E, X, C, pt = cx.X, cx.X, cx.C, cx.T
            kq_k, qkv_k = tk
            Em, Nf, Ntf, Dt, NoT, T1b, Dm, NP, Nt = cx.Em, cx.Nf, cx.Ntf, cx.Dt, cx.NoT, cx.T1b, cx.Dm, cx.NP, cx.Nt
            qkTm, kdec, Rb, ub = cx.qkTm, cx.kdec, cx.Rb, cx.ub
            mm(E[:, 256:512], kqT[:, lt, 0:128], kqT[:, lt, 0:256], True, True, [kq_k[lt]], [E.k])
            for half in range(2):
                mm(X[:, half * 128:(half + 1) * 128], self_[:, c, :], GT[:, tile_g, :], True, False, [self_.k, GT.k], [X.k])
                mm(X[:, half * 128:(half + 1) * 128], identb[:], negmb[:, d, half * 128:(half + 1) * 128], False, True,
                   [identb.k, negmb.k], [X.k])
            yield
            act(Em[:], X[:, 0:256], AF.Exp, [nG.k], [X.k, Em.k], bias=nG[:, tile_g, c:c + 1])
            yield
            stt("dve", Nf[:], E[:, 256:384], nbeta[:, tile_g, c:c + 1], Em[:, 128:256], ALU.mult, ALU.mult,
                [nbeta.k, Em.k], [E.k, Nf.k])
            tt("dve", qkTm[:], E[:, 384:512], Em[:, 0:128], ALU.mult, [Em.k], [E.k, qkTm.k])
            ts("pool", kdec[:], qkv[:, lt, 128:256], rn[:, lt, 1:2], kds[:, tile_g, c:c + 1], ALU.mult, ALU.mult,
               [qkv_k[lt], rn.k, kds.k], [kdec.k])
            yield
            tr(pt[:, 0:128], Nf[:], identb[:], [Nf.k, identb.k], [pt.k])
            cp("act", Ntf[:], pt[:, 0:128], [], [pt.k, Ntf.k])
            tt("pool", NP[0][:, 0:128], Nf[:], bmk[:, 0, :], ALU.mult, [Nf.k, bmk.k], [NP[0].k])
            cp("pool", NP[0][:, 128:256], identb[:], [identb.k], [NP[0].k])
            yield
            tt("pool", Nt[0][:], Ntf[:], bmk[:, 0, :], ALU.mult, [Ntf.k, bmk.k], [Nt[0].k])
            yield
            cur = 0
            for k in range(3):
                np_c, nt_c = NP[cur], Nt[cur]
                np_n, nt_n = NP[1 - cur], Nt[1 - cur]
                if k < 2:
                    mm(E[:, 0:256], nt_c[:], np_c[:, 0:256], True, False, [nt_c.k, np_c.k], [E.k])
                    mm(E[:, 128:256], identb[:], np_c[:, 128:256], False, True, [identb.k, np_c.k], [E.k])
                    mm(X[:, 256:384], np_c[:, 0:128], nt_c[:], True, True, [np_c.k, nt_c.k], [X.k])
                    yield
                    cp("dve", np_n[:, 0:256], E[:, 0:256], [], [E.k, np_n.k])
                    cp("act", nt_n[:], X[:, 256:384], [], [X.k, nt_n.k])
                    yield
                else:
                    mm(E[:, 128:256], nt_c[:], np_c[:, 128:256], True, False, [nt_c.k, np_c.k], [E.k])
                    mm(E[:, 128:256], identb[:], np_c[:, 128:256], False, True, [identb.k, np_c.k], [E.k])
                    yield
                    cp("dve", Dm[0][:], E[:, 128:256], [], [E.k, Dm[0].k])
                    yield
                cur = 1 - cur
            dcur = 0
            for li in range(4):
                Dc, Dn = Dm[dcur], Dm[1 - dcur]
                tr(pt[:, 0:128], Dc[:], identb[:], [Dc.k, identb.k], [pt.k])
                cp("act", Dt[:], pt[:, 0:128], [], [pt.k, Dt.k])
                tt("pool", NoT[:], Ntf[:], bmk[:, 1 + li, :], ALU.mult, [Ntf.k, bmk.k], [NoT.k])
                yield
                mm(E[:, 0:128], NoT[:], Dc[:], True, True, [NoT.k, Dc.k], [E.k])
                yield
                cp("act", T1b[:], E[:, 0:128], [], [E.k, T1b.k])
                yield
                mm(E[:, 128:256], Dt[:], T1b[:], True, True, [Dt.k, T1b.k], [E.k])
                yield
                tt("dve", Dn[:], E[:, 128:256], Dc[:], ALU.add, [Dc.k], [E.k, Dn.k])
                yield
                dcur = 1 - dcur
            MT = Dm[dcur]
            S_f, S_b = Sf[d], Sb[d]
            mm(C[:, 0:128], kqT[:, lt, 0:128], S_b[:], True, True, [kq_k[lt], S_b.k], [C.k])
            mm(C[:, 128:256], kqT[:, lt, 128:256], S_b[:], True, True, [kq_k[lt], S_b.k], [C.k])
            yield
            stt("dve", Rb[:], C[:, 0:128], eG[:, tile_g, c:c + 1], qkv[:, lt, 256:384], ALU.mult, ALU.subtract,
                [eG.k, qkv_k[lt]], [C.k, Rb.k])
            stt("dve", oacc[:, lt, :], C[:, 128:256], eG[:, tile_g, c:c + 1], oacc[:, lt, :], ALU.mult, ALU.add,
                [eG.k], [C.k, oacc_k[lt]])
            yield
            mm(C[:, 256:384], MT[:], Rb[:], True, True, [MT.k, Rb.k], [C.k])
            yield
            act(ub[:], C[:, 256:384], AF.Copy, [nbeta.k], [C.k, ub.k], scale=nbeta[:, tile_g, c:c + 1])
            yield
            mm(C[:, 0:128], qkTm[:], ub[:], True, True, [qkTm.k, ub.k], [C.k])
            mm(C[:, 128:256], kdec[:], ub[:], True, True, [kdec.k, ub.k], [C.k])
            yield
            tt("dve", oacc[:, lt, :], C[:, 0:128], oacc[:, lt, :], ALU.add, [], [C.k, oacc_k[lt]])
            stt("dve", S_f[:], S_f[:], gl[:, tile_g, c:c + 1], C[:, 128:256], ALU.mult, ALU.add, [gl.k], [C.k, S_f.k])
            yield
            cp("act", S_b[:], S_f[:], [S_f.k], [S_b.k])
            yield

        def mk_gdn_ctx(X, E, C, T):
            cx = Ctx()
            cx.X, cx.E, cx.C, cx.T = X, E, C, T
            cx.Em = B(P, [128, 256])
            cx.Nf = B(P, [128, 128], BF16); cx.Ntf = B(P, [128, 128], BF16); cx.Dt = B(P, [128, 128], BF16)
            cx.NoT = B(P, [128, 128], BF16); cx.T1b = B(P, [128, 128], BF16)
            cx.Dm = [B(P, [128, 128], BF16) for i in range(2)]
            cx.NP = [B(P, [128, 256], BF16) for i in range(2)]
            cx.Nt = [B(P, [128, 128], BF16) for i in range(2)]
            cx.qkTm = B(P, [128, 128], BF16); cx.kdec = B(P, [128, 128], BF16)
            cx.Rb = B(P, [128, 128], BF16); cx.ub = B(P, [128, 128], BF16)
            return cx

        for h in range(8):
            wg = next_wb()
            for j in range(4):
                load_w(wg, W[:, j * 1024 + h * 128: j * 1024 + (h + 1) * 128], 128, col0=j * 128)
            wd = next_wb()
            for j in range(4):
                load_w(wd, W[:, 4128 + j * 1024 + h * 128: 4128 + j * 1024 + (h + 1) * 128], 128, col0=j * 128)
            for j in range(3):
                for tap in range(5):
                    ts("pool", dg[:, j * 5 + tap, :], identf[:], convw[:, j * 8 + h, tap:tap + 1], None, ALU.mult, None,
                       [identf.k, convw.k], [dg.k])
            for si, (t0, nt_, g) in enumerate(SEQS):
                T = nt_ * 128
                ctx = (g == 1)
                with P.phase():
                    xpad = B(P, [128, 3, 1028], BF16)
                    qkv = B(P, [128, 8, 384], BF16)
                    kqT = B(P, [128, 8, 256], BF16)
                    qkb = B(P, [128, 2, 128], BF16)
                    rn = B(P, [128, 8, 2])
                    zt = B(P, [128, 8, 128], BF16)
                    oacc = B(P, [128, 8, 128])
                    oacc_k = [Tok() for _ in range(8)]
                    osq = B(P, [128, 8, 128]); ofbb = B(P, [128, 8, 128], BF16)
                    cxs = [mk_gdn_ctx(psX, psX, psC[0], psT[0]), mk_gdn_ctx(psE, psE, psC[1], psT[1])]
                    kq_k = [Tok() for _ in range(8)]
                    qkv_k = [Tok() for _ in range(8)]
                    ready = set()
                    mset("pool", xpad[:, :, 0:2], 0.0, [xpad.k])
                    mset("pool", xpad[:, :, T + 2:T + 4], 0.0, [xpad.k])
                    mset("pool", oacc[:], 0.0, oacc_k)
                    for j in range(3):
                        for b0 in range(0, T, 512):
                            nb = min(512, T - b0)
                            pp = next_psP()
                            for kt in range(KT):
                                mm(pp[:, 0:nb], wg[:, kt, j * 128:(j + 1) * 128], hT[:, kt, t0 * 128 + b0:t0 * 128 + b0 + nb],
                                   kt == 0, kt == KT - 1, [wg.k, hT.k], [pp.k])
                            cp("act", xpad[:, j, 2 + b0:2 + b0 + nb], pp[:, 0:nb], [], [pp.k, xpad.k])
                    def prologue():
                        order = []
                        for i_ in range((nt_ + 1) // 2):
                            order.append(i_)
                            if nt_ - 1 - i_ != i_:
                                order.append(nt_ - 1 - i_)
                        for lt in order:
                            tg = t0 + lt
                            pp = next_psP()
                            for kt in range(KT):
                                mm(pp[:, 0:128], hT[:, kt, tg * 128:(tg + 1) * 128], wg[:, kt, 384:512], kt == 0, kt == KT - 1,
                                   [hT.k, wg.k], [pp.k])
                            act(zt[:, lt, :], pp[:, 0:128], AF.Silu, [], [pp.k, zt.k])
                            yield
                            pc_ = next_psP()
                            for j in range(3):
                                for tap in range(5):
                                    mm(pc_[:, j * 128:(j + 1) * 128], xpad[:, j, lt * 128 + tap:lt * 128 + tap + 128], dg[:, j * 5 + tap, :],
                                       tap == 0, tap == 4, [xpad.k, dg.k], [pc_.k])
                            yield
                            act(qkv[:, lt, :], pc_[:, 0:384], AF.Silu, [], [pc_.k, qkv_k[lt]])
                            yield
                            act(qkb[:, 0, :], qkv[:, lt, 0:128], AF.Square, [qkv_k[lt]], [qkb.k, rn.k], accum_out=rn[:, lt, 0:1])
                            act(qkb[:, 1, :], qkv[:, lt, 128:256], AF.Square, [qkv_k[lt]], [qkb.k, rn.k], accum_out=rn[:, lt, 1:2])
                            rsq(rn[:, lt, :], rn[:, lt, :], rn[:, lt, :], 1.0, [rn.k, epsb.k])
                            yield
                            ts("dve", qkb[:, 0, :], qkv[:, lt, 128:256], rn[:, lt, 1:2], None, ALU.mult, None, [qkv_k[lt], rn.k], [qkb.k])
                            ts("dve", qkb[:, 1, :], qkv[:, lt, 0:128], rn[:, lt, 0:1], 128.0 ** -0.5, ALU.mult, ALU.mult,
                               [qkv_k[lt], rn.k], [qkb.k])
                            yield
                            pt = psT[0]
                            tr(pt[:, 0:128], qkb[:, 0, :], identb[:], [qkb.k, identb.k], [pt.k])
                            tr(pt[:, 128:256], qkb[:, 1, :], identb[:], [qkb.k, identb.k], [pt.k])
                            cp("act", kqT[:, lt, :], pt[:, 0:256], [], [pt.k, kq_k[lt]])
                            ready.add(lt)
                            yield
                    for d in range(2):
                        if ctx:
                            P.dma("sp", Sf[d][:], I["sgdn"][e, d, h], writes=[Sf[d].k])
                            cp("act", Sb[d][:], Sf[d][:], [Sf[d].k], [Sb[d].k])
                        else:
                            mset("pool", Sf[d][:], 0.0, [Sf[d].k])
                            mset("pool", Sb[d][:], 0.0, [Sb[d].k])

                    def chain(d, cx):
                        order = range(nt_) if d == 0 else reversed(range(nt_))
                        for lt in order:
                            while lt not in ready:
                                yield
                            yield from gdn_unit(cx, qkv, kqT, rn, oacc, oacc_k, (kq_k, qkv_k), t0 + lt, lt, h, d)

                    interleave([prologue(), chain(0, cxs[0]), chain(1, cxs[1])], [0, 0, 9])
                    if not ctx:
                        for d in range(2):
                            r = P.dma("sp", O["o_gdn"][si, e, d, h], Sf[d][:], reads=[Sf[d].k])
                            out_refs.append(r)
                    finish_seq(oacc, list(oacc_k[0:nt_]), gdn_gb, zt, zt.k, h, t0, nt_, osq, ofbb)
                with P.phase():
                    qkvd = B(P, [128, 512]); qkr = B(P, [128, 256]); qkdb = B(P, [128, 256], BF16)
                    QT = B(P, [128, 1024], BF16); KTb = B(P, [128, 1280], BF16); Vb = B(P, [128, 10, 128], BF16)
                    zd = B(P, [128, 8, 128], BF16); ckb = B(P, [128, 2, 128], BF16)
                    PT = [B(P, [128, 512], BF16) for i in range(2)]
                    rope_t = [B(P, [128, 128]) for i in range(2)]
                    oatt = B(P, [128, 8, 128]); osq = B(P, [128, 8, 128]); ofbb = B(P, [128, 8, 128], BF16)
                    rot = B(P, [128, 256])
                    rot5 = rot[:].rearrange("p (a q w c) -> p a q w c", a=4, q=2, w=2)
                    nk = nt_ + (2 if ctx else 0)
                    for lt in range(nt_):
                        tg = t0 + lt
                        pp = next_psP()
                        for kt in range(KT):
                            mm(pp[:], hT[:, kt, tg * 128:(tg + 1) * 128], wd[:, kt, :], kt == 0, kt == KT - 1, [hT.k, wd.k], [pp.k])
                        cp("act", qkvd[:, 0:384], pp[:, 0:384], [], [pp.k, qkvd.k])
                        act(zd[:, lt, :], pp[:, 384:512], AF.Silu, [], [pp.k, zd.k])
                        cp("pool", Vb[:, lt, :], qkvd[:, 256:384], [qkvd.k], [Vb.k])
                        if not ctx:
                            r = P.dma("sp", O["o_ck"][si, e, lt * 128:(lt + 1) * 128, h, :], qkvd[:, 128:256], reads=[qkvd.k])
                            out_refs.append(r)
                            r = P.dma("sp", O["o_cv"][si, e, lt * 128:(lt + 1) * 128, h, :], qkvd[:, 256:384], reads=[qkvd.k])
                            out_refs.append(r)
                            cp("dve", qkdb[:], qkvd[:, 0:256], [qkvd.k], [qkdb.k])
                        else:
                            rope = rope_t[lt % 2]
                            P.dma("sp", rope[:], I["k_rope"][lt * 128:(lt + 1) * 128].rearrange("p a b -> p (a b)"), writes=[rope.k])
                            xv = qkvd[:, 0:256].rearrange("p (a b) -> p a b", b=64)
                            rv = qkr[:].rearrange("p (a b) -> p a b", b=64)
                            cosb = rope[:, 0:64].unsqueeze(1).to_broadcast([128, 4, 64])
                            tt("dve", rv, xv, cosb, ALU.mult, [qkvd.k, rope.k], [qkr.k])
                            x5 = qkvd[:, 0:256].rearrange("p (a q w c) -> p a q w c", a=4, q=2, w=2)
                            s5 = rope[:, 64:128].rearrange("p (q w c) -> p q w c", q=2, w=2)
                            for w_ in range(2):
                                tt("dve", rot5[:, :, :, w_, :], x5[:, :, :, 1 - w_, :],
                                   s5[:, :, w_, :].unsqueeze(1).to_broadcast([128, 4, 2, 16]), ALU.mult, [qkvd.k, rope.k], [rot.k])
                            tt("dve", qkdb[:], qkr[:], rot[:], ALU.add, [qkr.k, rot.k], [qkdb.k])
                        pt = psT[0]
                        tr(pt[:, 0:128], qkdb[:, 0:128], identb[:], [qkdb.k, identb.k], [pt.k])
                        tr(pt[:, 128:256], qkdb[:, 128:256], identb[:], [qkdb.k, identb.k], [pt.k])
                        cp("act", QT[:, lt * 128:(lt + 1) * 128], pt[:, 0:128], [], [pt.k, QT.k])
                        cp("act", KTb[:, lt * 128:(lt + 1) * 128], pt[:, 128:256], [], [pt.k, KTb.k])
                    if ctx:
                        P.dma("pool", ckb[:], I["ck"][e, :, h, :].rearrange("(n p) d -> p n d", p=128), writes=[ckb.k])
                        P.dma("pool", Vb[:, 8:10, :], I["cv"][e, :, h, :].rearrange("(n p) d -> p n d", p=128), writes=[Vb.k])
                        pt = psT[0]
                        for n in range(2):
                            tr(pt[:, n * 128:(n + 1) * 128], ckb[:, n, :], identb[:], [ckb.k, identb.k], [pt.k])
                        cp("act", KTb[:, 1024:1280], pt[:, 0:256], [], [pt.k, KTb.k])
                    pi = 0
                    for q0 in range(0, T, 512):
                        nq = min(512, T - q0)
                        nqt = nq // 128
                        accA, accB, accS = psC[0], psC[1], psE
                        for kk in range(nk):
                            pts = []
                            for m in range(2):
                                pp = next_psP()
                                mm(pp[:, 0:nq], KTb[m * 64:(m + 1) * 64, kk * 128:(kk + 1) * 128], QT[m * 64:(m + 1) * 64, q0:q0 + nq],
                                   True, True, [KTb.k, QT.k], [pp.k])
                                p_ = PT[pi % 2]
                                pi += 1
                                act(p_[:, 0:nq], pp[:, 0:nq], AF.Exp, [], [pp.k, p_.k], scale=0.125)
                                pts.append(p_)
                            for m in range(2):
                                acc = accA if m == 0 else accB
                                for qt in range(nqt):
                                    mm(acc[:, qt * 128:(qt + 1) * 128], pts[m][:, qt * 128:(qt + 1) * 128], Vb[:, kk, :],
                                       kk == 0 and qt == 0, kk == nk - 1, [pts[m].k, Vb.k], [acc.k])
                                    mm(accS[:, qt * 2 + m:qt * 2 + m + 1], pts[m][:, qt * 128:(qt + 1) * 128], onesb[:, 0:1],
                                       kk == 0 and qt == 0 and m == 0, kk == nk - 1, [pts[m].k, onesb.k], [accS.k])
                        P.op("dve", lambda e_, nqt=nqt, accS=accS: e_.reciprocal(rs[:, 4:4 + 2 * nqt], accS[:, 0:2 * nqt]), reads=[], writes=[accS.k, rs.k])
                        for qt in range(nqt):
                            lt = q0 // 128 + qt
                            ts("dve", rs[:, 13:14], rs[:, 5 + 2 * qt:6 + 2 * qt], lamv[:, 3:4], None, ALU.mult, None, [lamv.k], [rs.k])
                            ts("dve", ofin[:], accA[:, qt * 128:(qt + 1) * 128], rs[:, 4 + 2 * qt:5 + 2 * qt], None, ALU.mult, None,
                               [rs.k], [accA.k, ofin.k])
                            stt("dve", oatt[:, lt, :], accB[:, qt * 128:(qt + 1) * 128], rs[:, 13:14], ofin[:], ALU.mult, ALU.add,
                                [rs.k, ofin.k], [accB.k, oatt.k])
                    finish_seq(oatt, [oatt.k], diff_gb, zd, zd.k, 8 + h, t0, nt_, osq, ofbb)

    def odd_layer(l):
      with P.phase():
        od = l // 2
        W = I["w_in_odd"][od]
        glrT = [B(P, [32, 1536], BF16) for r in range(2)]
        wgt = [B(P, [32, 1024], BF16) for r in range(2)]
        wglr = B(P, [128, KT, 32], BF16)
        load_w(wglr, W[:, 6144:6176], 32)
        for r in range(2):
            mset("pool", wgt[r][:], 0.0, [wgt[r].k])
            P.dma("pool", wgt[r][0:16, :], I["w_gate"][od, r], writes=[wgt[r].k])
            P.dma("pool", wgt[r][16:17, :], I["b_gate"][od, r:r + 1, :], writes=[wgt[r].k])
            mset("pool", glrT[r][:], 1.0, [glrT[r].k])
            for b0 in range(0, 1536, 512):
                pp = next_psP()
                for kt in range(KT):
                    mm(pp[0:16, :], wglr[:, kt, r * 16:(r + 1) * 16], hT[:, kt, b0:b0 + 512], kt == 0, kt == KT - 1,
                       [wglr.k, hT.k], [pp.k])
                cp("act", glrT[r][0:16, b0:b0 + 512], pp[0:16, :], [], [pp.k, glrT[r].k])
        with P.phase():
            qk = B(P, [128, 8, 512], BF16); vv = B(P, [128, 8, 256], BF16); zz = B(P, [128, 8, 256], BF16)
            Sg = [[B(P, [128, 256]) for i in range(2)] for d in range(2)]
            Sgb = [[B(P, [128, 256], BF16) for i in range(2)] for d in range(2)]
            og = B(P, [128, 8, 256])
            og_k = [Tok() for _ in range(8)]

            def mk_gla_ctx(X, E, C, T):
                cx = Ctx()
                cx.X, cx.E, cx.C, cx.T = X, E, C, T
                cx.la = B(P, [128, 256]); cx.Bc = B(P, [128, 256]); cx.Bl = B(P, [128, 256]); cx.ex = B(P, [128, 3, 256])
                cx.qt_ = B(P, [128, 256], BF16); cx.kt_ = B(P, [128, 256], BF16); cx.kd = B(P, [128, 256], BF16)
                cx.qkTT = B(P, [128, 2, 256], BF16)
                cx.ATm = B(P, [128, 128], BF16)
                cx.blc = B(P, [128, 2])
                return cx

            cxs = [mk_gla_ctx(psX, psE, psX, psT[0]), mk_gla_ctx(psC[0], psC[1], psC[0], psT[1])]
            qk_k = [Tok() for _ in range(8)]
            vv_k = [Tok() for _ in range(8)]

            def gla_unit(cx, tg, lt, h, d):
                X, E, C, pt = cx.X, cx.E, cx.C, cx.T
                la, Bc, Bl, ex, qt_, kt_, kd, qkTT, ATm, blc = cx.la, cx.Bc, cx.Bl, cx.ex, cx.qt_, cx.kt_, cx.kd, cx.qkTT, cx.ATm, cx.blc
                mm(X[:, 0:256], glrT[d][:, tg * 128:(tg + 1) * 128], wgt[d][:, h * 256:(h + 1) * 256], True, True,
                   [glrT[d].k, wgt[d].k], [X.k])
                yield
                act(la[:], X[:, 0:256], AF.Exp, [], [X.k, la.k], scale=-1.0)
                act(la[:], la[:], AF.Ln, [], [la.k], bias=1.0)
                yield
                ts("dve", la[:], la[:], -1.0 / 16.0, None, ALU.mult, None, [], [la.k])
                yield
                mm(X[:, 0:256], trif[:, d, :], la[:], True, True, [trif.k, la.k], [X.k])
                mm(X[:, 256:512], onesf[:], la[:], True, True, [onesf.k, la.k], [X.k])
                for dt_ in range(2):
                    mm(E[:, dt_:dt_ + 1], la[:, dt_ * 128:(dt_ + 1) * 128], onesf[:, 0:1], True, True, [la.k, onesf.k], [E.k])
                yield
                cp("dve", Bc[:], X[:, 0:256], [], [X.k, Bc.k])
                tt("dve", Bl[:], X[:, 256:512], Bc[:], ALU.subtract, [Bc.k], [X.k, Bl.k])
                act(blc[:], E[:, 0:2], AF.Exp, [], [E.k, blc.k])
                yield
                act(ex[:, 0, :], Bc[:], AF.Exp, [Bc.k], [ex.k])
                act(ex[:, 1, :], Bc[:], AF.Exp, [Bc.k], [ex.k], scale=-1.0)
                act(ex[:, 2, :], Bl[:], AF.Exp, [Bl.k], [ex.k])
                yield
                stt("dve", qt_[:], qk[:, lt, 0:256], 256.0 ** -0.5, ex[:, 0, :], ALU.mult, ALU.mult, [qk_k[lt], ex.k], [qt_.k])
                tt("pool", kt_[:], qk[:, lt, 256:512], ex[:, 1, :], ALU.mult, [qk_k[lt], ex.k], [kt_.k])
                tt("pool", kd[:], qk[:, lt, 256:512], ex[:, 2, :], ALU.mult, [qk_k[lt], ex.k], [kd.k])
                yield
                for dt_ in range(2):
                    tr(pt[:, dt_ * 256:dt_ * 256 + 128], kt_[:, dt_ * 128:(dt_ + 1) * 128], identb[:], [kt_.k, identb.k], [pt.k])
                    tr(pt[:, dt_ * 256 + 128:dt_ * 256 + 256], qt_[:, dt_ * 128:(dt_ + 1) * 128], identb[:], [qt_.k, identb.k], [pt.k])
                yield
                cp("act", qkTT[:].rearrange("p a b -> p (a b)"), pt[:, 0:512], [], [pt.k, qkTT.k])
                yield
                for dt_ in range(2):
                    mm(E[:, 128:256], qkTT[:, dt_, 0:128], qkTT[:, dt_, 128:256], dt_ == 0, dt_ == 1, [qkTT.k], [E.k])
                yield
                tt("dve", ATm[:], E[:, 128:256], m01[:, d, :], ALU.mult, [m01.k], [E.k, ATm.k])
                yield
                for dt_ in range(2):
                    mm(E[:, 256:512], qkTT[:, dt_, 128:256], Sgb[d][dt_][:], dt_ == 0, False, [qkTT.k, Sgb[d][dt_].k], [E.k])
                mm(E[:, 256:512], ATm[:], vv[:, lt, :], False, True, [ATm.k, vv_k[lt]], [E.k])
                yield
                tt("dve", og[:, lt, :], E[:, 256:512], og[:, lt, :], ALU.add, [], [E.k, og_k[lt]])
                yield
                for dt_ in range(2):
                    mm(C[:, 0:256], kd[:, dt_ * 128:(dt_ + 1) * 128], vv[:, lt, :], True, True, [kd.k, vv_k[lt]], [C.k])
                    yield
                    stt("dve", Sg[d][dt_][:], Sg[d][dt_][:], blc[:, dt_:dt_ + 1], C[:, 0:256], ALU.mult, ALU.add, [blc.k],
                        [C.k, Sg[d][dt_].k])
                    yield
                    cp("act", Sgb[d][dt_][:], Sg[d][dt_][:], [Sg[d][dt_].k], [Sgb[d][dt_].k])
                    yield

            for h in range(4):
                for e2 in range(2):
                    wq = next_wb()
                    load_w(wq, W[:, h * 256:(h + 1) * 256], 256, col0=0)
                    load_w(wq, W[:, 1024 + h * 256:1024 + (h + 1) * 256], 256, col0=256)
                    wv = next_wb()
                    load_w(wv, W[:, 2048 + h * 512 + e2 * 256:2048 + h * 512 + (e2 + 1) * 256], 256, col0=0)
                    load_w(wv, W[:, 4096 + h * 512 + e2 * 256:4096 + h * 512 + (e2 + 1) * 256], 256, col0=256)
                    for si, (t0, nt_, g) in enumerate(SEQS):
                        ctx = (g == 1)
                        ready = set()

                        def prologue():
                            order = []
                            for i_ in range((nt_ + 1) // 2):
                                order.append(i_)
                                if nt_ - 1 - i_ != i_:
                                    order.append(nt_ - 1 - i_)
                            for lt in order:
                                tg = t0 + lt
                                for wi, wsrc in enumerate((wq, wv)):
                                    pp = next_psP()
                                    for kt in range(KT):
                                        mm(pp[:], hT[:, kt, tg * 128:(tg + 1) * 128], wsrc[:, kt, :], kt == 0, kt == KT - 1, [hT.k, wsrc.k], [pp.k])
                                    yield
                                    if wi == 0:
                                        cp("act", qk[:, lt, :], pp[:], [], [pp.k, qk_k[lt]])
                                    else:
                                        cp("act", vv[:, lt, :], pp[:, 0:256], [], [pp.k, vv_k[lt]])
                                        act(zz[:, lt, :], pp[:, 256:512], AF.Silu, [], [pp.k, zz.k])
                                    yield
                                ready.add(lt)
                                yield

                        mset("pool", og[:], 0.0, og_k)
                        for d in range(2):
                            for dt_ in range(2):
                                if ctx:
                                    P.dma("sp", Sg[d][dt_][:], I["sgla"][od, d, h, dt_ * 128:(dt_ + 1) * 128, e2 * 256:(e2 + 1) * 256],
                                          writes=[Sg[d][dt_].k])
                                    cp("act", Sgb[d][dt_][:], Sg[d][dt_][:], [Sg[d][dt_].k], [Sgb[d][dt_].k])
                                else:
                                    mset("pool", Sg[d][dt_][:], 0.0, [Sg[d][dt_].k])
                                    mset("pool", Sgb[d][dt_][:], 0.0, [Sgb[d][dt_].k])

                        def chain(d, cx):
                            order = range(nt_) if d == 0 else reversed(range(nt_))
                            for lt in order:
                                while lt not in ready:
                                    yield
                                yield from gla_unit(cx, t0 + lt, lt, h, d)

                        interleave([prologue(), chain(0, cxs[0]), chain(1, cxs[1])], [0, 0, 7])
                        if not ctx:
                            for d in range(2):
                                for dt_ in range(2):
                                    r = P.dma("sp", O["o_gla"][si, od, d, h, dt_ * 128:(dt_ + 1) * 128, e2 * 256:(e2 + 1) * 256],
                                              Sg[d][dt_][:], reads=[Sg[d][dt_].k])
                                    out_refs.append(r)
                        for lt in range(nt_):
                            P.dma("sp", gbuf[(t0 + lt) * 128:(t0 + lt + 1) * 128, h, e2 * 256:(e2 + 1) * 256], og[:, lt, :],
                                  reads=[og_k[lt]], writes=[gbuf_k[t0 + lt]])
                            P.dma("sp", zbuf[(t0 + lt) * 128:(t0 + lt + 1) * 128, h, e2 * 256:(e2 + 1) * 256], zz[:, lt, :],
                                  reads=[zz.k], writes=[zbuf_k[t0 + lt]])
        with P.phase():
            gfull = B(P, [128, 512])
            ofl = [B(P, [128, 4, 512]) for i in range(2)]
            zfl = [B(P, [128, 4, 512], BF16) for i in range(2)]
            osq2 = B(P, [128, 4, 512]); ofb2 = B(P, [128, 4, 512], BF16)
            P.dma("sp", gfull[:], I["gla_norm"][od].partition_broadcast(128), writes=[gfull.k])
            for tg in range(NTT):
                of_, zf_ = ofl[tg % 2], zfl[tg % 2]
                P.dma("sp", of_[:], gbuf[tg * 128:(tg + 1) * 128, :, :], reads=[gbuf_k[tg]], writes=[of_.k])
                P.dma("sp", zf_[:], zbuf[tg * 128:(tg + 1) * 128, :, :], reads=[zbuf_k[tg]], writes=[zf_.k])
                tt("dve", osq2[:], of_[:], of_[:], ALU.mult, [of_.k], [osq2.k])
                red("dve", rs[:, 4:8], osq2[:], [osq2.k], [rs.k])
                act(rs[:, 4:8], rs[:, 4:8], AF.Ln, [], [rs.k, epsb.k], scale=1.0 / 512, bias=epsb[:, 0:1])
                act(rs[:, 4:8], rs[:, 4:8], AF.Exp, [], [rs.k], scale=-0.5)
                tt("dve", osq2[:], of_[:], rs[:, 4:8].unsqueeze(2).to_broadcast([128, 4, 512]), ALU.mult, [of_.k, rs.k], [osq2.k])
                tt("pool", osq2[:], osq2[:], gfull[:].unsqueeze(1).to_broadcast([128, 4, 512]), ALU.mult, [gfull.k], [osq2.k])
                tt("dve", ofb2[:], osq2[:], zf_[:], ALU.mult, [osq2.k, zf_.k], [ofb2.k])
                fl = ofb2[:].rearrange("p a b -> p (a b)")
                for half in range(2):
                    pt = psT[half]
                    for j in range(8):
                        kt = half * 8 + j
                        tr(pt[:, j * 128:(j + 1) * 128], fl[:, kt * 128:(kt + 1) * 128], identb[:], [ofb2.k, identb.k], [pt.k])
                    cp("act", oT[:, half * 8:(half + 1) * 8, tg * 128:(tg + 1) * 128], pt[:, 0:1024].rearrange("p (a b) -> p a b", b=128),
                       [], [pt.k, oT.k])

    gbuf = nc.dram_tensor("gbuf", [1536, 4, 512], F32, kind="Internal").ap()
    zbuf = nc.dram_tensor("zbuf", [1536, 4, 512], BF16, kind="Internal").ap()
    gbuf_k = [Tok() for _ in range(NTT)]
    zbuf_k = [Tok() for _ in range(NTT)]

    for l in range(n_layers):
        adaln(l)
        build_hT(l)
        if l % 2 == 0:
            even_layer(l)
        else:
            odd_layer(l)
        if dbg and l < 2:
            out_refs.append(P.dma("sp", O[f"dbg_hT{l}"], hT[:], reads=[hT.k]))
            out_refs.append(P.dma("sp", O[f"dbg_oT{l}"], oT[:], reads=[oT.k]))
        out_proj(l, I["w_out_even"][l // 2] if l % 2 == 0 else I["w_out_odd"][l // 2])
    P.wait("sp", out_refs)
    P.emit()
    return nc, P


def make_in_maps(inp):
    consts = host_consts()
    f = lambda a: np.ascontiguousarray(np.asarray(a, dtype=np.float32))
    shared = {
        "w_ada": f(inp["w_ada"]),
        "b_adaT": f(inp["b_ada"].reshape(4, 48, 128).transpose(0, 2, 1)),
        "b_ada": f(inp["b_ada"]),
        "npreT": f(inp["norm_pre"].reshape(4, 16, 128).transpose(0, 2, 1)),
        "npost": f(inp["norm_post"]),
        "w_in_even": f(inp["w_in_even"]),
        "convT": f(inp["conv_even"].reshape(2, 5, 24, 128).transpose(0, 3, 2, 1)),
        "a_log": f(inp["a_log_even"].reshape(2, 16)),
        "dt_bias": f(inp["dt_bias_even"].reshape(2, 16)),
        "gdn_norm": f(inp["gdn_norm_even"]),
        "lam": f(inp["lam_even"].reshape(2, 256)),
        "diff_norm": f(inp["diff_norm_even"]),
        "w_out_even": f(inp["w_out_even"]),
        "w_in_odd": f(inp["w_in_odd"]),
        "w_gate": f(inp["w_gate_odd"]),
        "b_gate": f(inp["b_gate_odd"]),
        "gla_norm": f(inp["gla_norm_odd"]),
        "w_out_odd": f(inp["w_out_odd"]),
    }
    shared.update(consts)
    maps = []
    for c in range(8):
        bs = c % 2
        m = dict(shared)
        m["xp"] = f(inp["x_prompt"][2 * c:2 * c + 2].reshape(512, D))
        m["xs"] = f(inp["x_sample"][bs])
        cv2 = np.stack([np.asarray(inp["c_ctx"]), np.asarray(inp["c"][bs])], axis=0)
        m["cT"] = f(cv2.reshape(2, 16, 128).transpose(2, 1, 0))
        m["sgdn"] = f(inp["state_gdn"][bs])
        m["ck"] = f(inp["cache_k"][bs])
        m["cv"] = f(inp["cache_v"][bs])
        m["sgla"] = f(inp["state_gla"][bs])
        maps.append(m)
    return maps


_CACHE = {}


def kernel(**inputs):
    if "nc" not in _CACHE:
        _CACHE["nc"] = build()[0]
    nc = _CACHE["nc"]
    maps = make_in_maps(inputs)
    res = run_bass_kernel_spmd(nc, maps, core_ids=list(range(8)))
    R = res.results
    y_p = np.concatenate([R[c]["yp"].reshape(2, 256, D) for c in range(8)], axis=0)
    y_s = np.stack([R[0]["ys"], R[1]["ys"]], axis=0)
    gdn = np.concatenate([R[c]["o_gdn"] for c in range(8)], axis=0)
    ck = np.concatenate([R[c]["o_ck"] for c in range(8)], axis=0)
    cv = np.concatenate([R[c]["o_cv"] for c in range(8)], axis=0)
    gla = np.concatenate([R[c]["o_gla"] for c in range(8)], axis=0)
    return (y_p.astype(np.float32), y_s.astype(np.float32), gdn.astype(np.float32), ck.astype(np.float32),
            cv.astype(np.float32), gla.astype(np.float32))
```
